# Optimizing a Trainium2 kernel written in Bass

```python
import math
import jax, jax.numpy as jnp
from jax import lax
import numpy as np

D_MODEL = 2048
BATCH = 1
SEQ = 8192
DEPTH = 1

D_MIX = D_MODEL
HEAD_DIM = 64
D_ATTN = D_MIX // 2
D_CONV = D_MIX - D_ATTN
N_Q_HEADS = D_ATTN // HEAD_DIM
N_KV_HEADS = 4
GQA_GROUP = N_Q_HEADS // N_KV_HEADS
D_KV = N_KV_HEADS * HEAD_DIM
WINDOW = 128
BLOCK = 128
CONV_WIDTH = 31
CONV_GROUPS = D_CONV // HEAD_DIM
N_BUCKETS = 32
MAX_DISTANCE = 128
LN_EPS = 1e-5
ALPHA = (2.0 * DEPTH) ** 0.25
BETA = (8.0 * DEPTH) ** -0.25

SPLIT_SIZES = (D_ATTN, D_KV, D_KV, D_ATTN, D_CONV, D_CONV, D_CONV)
D_IN = sum(SPLIT_SIZES)
SPLIT_POINTS = [int(s) for s in np.cumsum(SPLIT_SIZES)[:-1]]

kernel_name = "hybrid_conformer_swa_sink_deepnorm_adaln"


def layer_norm(x, g, b):
    xf = x.astype(jnp.float32)
    mu = jnp.mean(xf, axis=-1, keepdims=True)
    var = jnp.mean(jnp.square(xf - mu), axis=-1, keepdims=True)
    y = (xf - mu) * lax.rsqrt(var + LN_EPS)
    return (y * g.astype(jnp.float32) + b.astype(jnp.float32)).astype(x.dtype)


def t5_bucket(dist):
    max_exact = N_BUCKETS // 2
    d = jnp.maximum(dist, 1).astype(jnp.float32)
    large = max_exact + (jnp.log(d / max_exact) / math.log(MAX_DISTANCE / max_exact)
                         * (N_BUCKETS - max_exact)).astype(jnp.int32)
    large = jnp.minimum(large, N_BUCKETS - 1)
    return jnp.where(dist < max_exact, dist, large)


def banded_sink_attention(q, k, v, rel_bias, sinks):
    B, S = q.shape[0], q.shape[1]
    nb = S // BLOCK
    f32 = jnp.float32
    qb = q.astype(f32).reshape(B, nb, BLOCK, N_KV_HEADS, GQA_GROUP, HEAD_DIM)
    kb = k.astype(f32).reshape(B, nb, BLOCK, N_KV_HEADS, HEAD_DIM)
    vb = v.astype(f32).reshape(B, nb, BLOCK, N_KV_HEADS, HEAD_DIM)
    prev = lambda t: jnp.concatenate([jnp.zeros_like(t[:, :1]), t[:, :-1]], axis=1)
    kw = jnp.concatenate([prev(kb), kb], axis=2)
    vw = jnp.concatenate([prev(vb), vb], axis=2)
    scores = jnp.einsum('bnqhgd,bnkhd->bnhgqk', qb, kw) * (HEAD_DIM ** -0.5)

    qi = jnp.arange(BLOCK, dtype=jnp.int32)[:, None]
    kj = jnp.arange(2 * BLOCK, dtype=jnp.int32)[None, :]
    dist = qi + BLOCK - kj
    in_window = (dist >= 0) & (dist < WINDOW)
    bias = rel_bias.astype(f32)[t5_bucket(jnp.maximum(dist, 0))]
    bias = bias.transpose(2, 0, 1).reshape(N_KV_HEADS, GQA_GROUP, BLOCK, 2 * BLOCK)
    key_pos = jnp.arange(nb, dtype=jnp.int32)[:, None] * BLOCK - BLOCK + kj
    valid = in_window[None] & (key_pos[:, None, :] >= 0)

    scores = jnp.where(valid[None, :, None, None], scores + bias, jnp.finfo(f32).min)
    sink = jnp.broadcast_to(sinks.astype(f32).reshape(1, 1, N_KV_HEADS, GQA_GROUP, 1, 1),
                            scores.shape[:-1] + (1,))
    probs = jax.nn.softmax(jnp.concatenate([scores, sink], axis=-1), axis=-1)[..., :-1]
    out = jnp.einsum('bnhgqk,bnkhd->bnqhgd', probs, vw)
    return out.reshape(B, S, N_Q_HEADS * HEAD_DIM).astype(q.dtype)


def conformer_conv(glu_a, glu_b, conv_w, conv_b, ln_g, ln_b, w_pw, b_pw):
    u = glu_a * jax.nn.sigmoid(glu_b)
    u = lax.conv_general_dilated(
        u, conv_w.reshape(CONV_WIDTH, 1, D_CONV).astype(u.dtype),
        window_strides=(1,), padding=[(CONV_WIDTH - 1, 0)],
        dimension_numbers=('NWC', 'WIO', 'NWC'), feature_group_count=D_CONV) + conv_b
    u = jax.nn.silu(layer_norm(u, ln_g, ln_b))
    return u @ w_pw + b_pw


def setup_inputs(seed: int = 0) -> dict:
    key = jax.random.key(seed)
    ks = jax.random.split(key, 20)
    n = jax.random.normal
    f32 = jnp.float32
    x = n(ks[0], (BATCH, SEQ, D_MODEL), f32)
    c = n(ks[1], (BATCH, D_MODEL), f32)
    w_ada = 0.5 * D_MODEL ** -0.5 * n(ks[2], (DEPTH, D_MODEL, 3 * D_MODEL), f32)
    b_ada = 0.01 * n(ks[3], (DEPTH, 3 * D_MODEL), f32)
    col_scale = jnp.ones((D_IN,), f32).at[D_ATTN + D_KV:D_ATTN + 2 * D_KV].set(BETA)
    w_in = D_MODEL ** -0.5 * n(ks[4], (DEPTH, D_MODEL, D_IN), f32) * col_scale
    rel_bias = 0.5 * n(ks[5], (N_BUCKETS, N_Q_HEADS), f32)
    sinks = n(ks[6], (DEPTH, N_Q_HEADS), f32)
    conv_w = CONV_WIDTH ** -0.5 * n(ks[7], (DEPTH, CONV_WIDTH, D_CONV), f32)
    conv_b = 0.01 * n(ks[8], (DEPTH, D_CONV), f32)
    conv_ln_g = 1.0 + 0.01 * n(ks[9], (DEPTH, D_CONV), f32)
    conv_ln_b = 0.01 * n(ks[10], (DEPTH, D_CONV), f32)
    w_pw = BETA * D_CONV ** -0.5 * n(ks[11], (DEPTH, D_CONV, D_CONV), f32)
    b_pw = 0.01 * n(ks[12], (DEPTH, D_CONV), f32)
    w_out = BETA * D_MIX ** -0.5 * n(ks[13], (DEPTH, D_MIX, D_MODEL), f32)
    ln_g = 1.0 + 0.01 * n(ks[14], (DEPTH, D_MODEL), f32)
    ln_b = 0.01 * n(ks[15], (DEPTH, D_MODEL), f32)
    return {"x": x, "c": c, "w_ada": w_ada, "b_ada": b_ada, "w_in": w_in,
            "rel_bias": rel_bias, "sinks": sinks, "conv_w": conv_w, "conv_b": conv_b,
            "conv_ln_g": conv_ln_g, "conv_ln_b": conv_ln_b, "w_pw": w_pw, "b_pw": b_pw,
            "w_out": w_out, "ln_g": ln_g, "ln_b": ln_b}


def reference(x, c, w_ada, b_ada, w_in, rel_bias, sinks, conv_w, conv_b, conv_ln_g,
              conv_ln_b, w_pw, b_pw, w_out, ln_g, ln_b):
    c_act = jax.nn.silu(c)
    for l in range(DEPTH):
        mod = (c_act @ w_ada[l] + b_ada[l])[:, None, :]
        shift, scale, gate = jnp.split(mod, 3, axis=-1)
        h = x * (1.0 + scale) + shift
        proj = h @ w_in[l]
        q, k, v, g_attn, glu_a, glu_b, g_conv = jnp.split(proj, SPLIT_POINTS, axis=-1)
        y_attn = banded_sink_attention(q, k, v, rel_bias, sinks[l]) * jax.nn.silu(g_attn)
        y_conv = conformer_conv(glu_a, glu_b, conv_w[l], conv_b[l], conv_ln_g[l], conv_ln_b[l],
                                w_pw[l], b_pw[l]) * jax.nn.silu(g_conv)
        y = jnp.concatenate([y_attn, y_conv], axis=-1) @ w_out[l]
        x = layer_norm(ALPHA * x + gate * y, ln_g[l], ln_b[l])
    return x
```

```python
import os
import contextlib
import numpy as np
import concourse.bass as bass
import concourse.mybir as mybir
from concourse.bass_utils import run_bass_kernel_spmd

F32 = mybir.dt.float32
BF16 = mybir.dt.bfloat16
AF = mybir.ActivationFunctionType
ALU = mybir.AluOpType

D = 2048
SEQ = 8192
NCORES = 8
T = SEQ // NCORES
HALO = 128
TT = T + HALO
KC = D // 128
D_IN = 5632
NEG = -30000.0
ALPHA = 2.0 ** 0.25
EPS = 1e-5
NU = 4
UW = T + 32
NWB = 3
NSA = 24
NSA0 = 16
CONV_PER_UNIT = 3
CONV_DVE_SHARE = 0.85

SLABS = [("G", 0), ("Q", 0), ("M", 0), ("Q", 1), ("G", 1), ("M", 1), ("K", 0), ("V", 0),
         ("G", 2), ("A", 0), ("A", 1), ("M", 2), ("G", 3), ("Q", 2), ("Q", 3), ("M", 3),
         ("G", 4), ("G", 5), ("M", 4), ("A", 2), ("A", 3), ("M", 5), ("G", 6), ("M", 6),
         ("G", 7), ("M", 7), ("C", 0), ("C", 1), ("C", 2), ("C", 3)]
WSLABS = [s_ for s_ in SLABS if s_[0] != "M"]

_ESZ = {}


def esize(dt):
    if dt not in _ESZ:
        _ESZ[dt] = int(np.dtype(mybir.dt.np(dt)).itemsize)
    return _ESZ[dt]


class Prog:
    STREAMS = ("pe", "act", "dve", "pool", "sp")

    def __init__(self, nc, n_dma_sems=12):
        self.nc = nc
        self.ops = []
        self.base = {}
        self.n_dma_sems = n_dma_sems

    def sb(self, name, shape, dtype, offset):
        t = self.nc.alloc_sbuf_tensor_at(name, list(shape), dtype, offset=offset)
        self.base[t.name] = ("sb", offset)
        return t

    def psum(self, name, shape, dtype):
        t = self.nc.alloc_psum_tensor(name, list(shape), dtype)
        self.base[t.name] = ("ps", self.nc.lookup_mloc(t).addr)
        return t

    def dram(self, name, shape, dtype, kind="Internal"):
        t = self.nc.dram_tensor(name, list(shape), dtype, kind=kind)
        self.base[t.name] = ("dram:" + t.name, 0)
        return t

    def region(self, ap):
        space, base = self.base[ap.tensor.name]
        es = esize(ap.dtype)
        pairs = [tuple(p) for p in ap.ap]
        off = int(ap.offset)
        if space in ("sb", "ps"):
            pstride = int(np.prod(list(ap.tensor.shape)[1:]))
            foff = off % pstride
            ext = 1
            for st, cnt in pairs[1:]:
                ext += (cnt - 1) * abs(st)
            lo = base + foff * es
            return (space, lo, lo + ext * es)
        ext = 1
        for st, cnt in pairs:
            ext += (cnt - 1) * abs(st)
        return (space, off * es, (off + ext) * es)

    def op(self, stream, fn, reads=(), writes=(), dma=False, name=""):
        rr, ww = [], []
        for lst, is_w in ((reads, False), (writes, True)):
            for a in lst:
                sp, lo, hi = self.region(a)
                if sp == "ps":
                    assert lo // 2048 == (hi - 1) // 2048, ("psum access crosses a bank", lo, hi)
                    ww.append((sp, lo // 2048 * 2048, lo // 2048 * 2048 + 2048))
                elif is_w:
                    ww.append((sp, lo, hi))
                else:
                    rr.append((sp, lo, hi))
        self.ops.append(dict(stream=stream, fn=fn, reads=rr, writes=ww, dma=dma,
                             name=name, deps=set(), idx=len(self.ops)))
        return len(self.ops) - 1

    def _analyze(self):
        segs = {}

        def touch(space, lo, hi, opi, is_write, deps):
            lst = segs.setdefault(space, [])
            new, out = [], []
            cur = lo
            for s in lst:
                slo, shi, w, rd = s
                if shi <= lo or slo >= hi:
                    out.append(s)
                    continue
                if slo < lo:
                    out.append([slo, lo, w, list(rd)])
                a, b = max(slo, lo), min(shi, hi)
                if cur < a:
                    new.append([cur, a, None, []])
                new.append([a, b, w, list(rd)])
                cur = b
                if shi > hi:
                    out.append([hi, shi, w, list(rd)])
            if cur < hi:
                new.append([cur, hi, None, []])
            for s in new:
                if is_write:
                    if s[2] is not None:
                        deps.add((s[2], "waw"))
                    for r in s[3]:
                        deps.add((r, "war"))
                    s[2], s[3] = opi, []
                else:
                    if s[2] is not None:
                        deps.add((s[2], "raw"))
                    if opi not in s[3]:
                        s[3].append(opi)
            out.extend(new)
            out.sort(key=lambda x: x[0])
            merged = []
            for s in out:
                if merged and merged[-1][1] == s[0] and merged[-1][2] == s[2] and merged[-1][3] == s[3]:
                    merged[-1][1] = s[1]
                else:
                    merged.append(s)
            segs[space] = merged

        for o in self.ops:
            deps = set()
            for (sp, lo, hi) in o["reads"]:
                touch(sp, lo, hi, o["idx"], False, deps)
            for (sp, lo, hi) in o["writes"]:
                touch(sp, lo, hi, o["idx"], True, deps)
            final = set()
            for (d, kind) in deps:
                if d == o["idx"]:
                    continue
                do = self.ops[d]
                if (not do["dma"]) and (not o["dma"]) and do["stream"] == o["stream"]:
                    if o["stream"] == "pe":
                        continue
                final.add(d)
            o["deps"] = final

    def emit(self):
        nc = self.nc
        self._analyze()
        needed = set()
        for o in self.ops:
            needed |= o["deps"]
        stack = contextlib.ExitStack()
        sems = {s: stack.enter_context(nc.semaphore("sem_" + s)) for s in self.STREAMS}
        dma_sems = {}
        for s in self.STREAMS:
            if any(o["dma"] and o["stream"] == s for o in self.ops):
                dma_sems[s] = [stack.enter_context(nc.semaphore("dsem_%s_%d" % (s, i)))
                               for i in range(self.n_dma_sems)]
        cnt = {s: 0 for s in self.STREAMS}
        dcount = {s: [0] * self.n_dma_sems for s in dma_sems}
        dnext = {s: 0 for s in dma_sems}
        for o in self.ops:
            st = o["stream"]
            if o["dma"]:
                k = dnext[st] % self.n_dma_sems
                dnext[st] += 1
                o["prev_val"] = dcount[st][k]
                dcount[st][k] += 16
                o["sig"] = (("d", st, k), dcount[st][k])
            elif o["idx"] in needed:
                cnt[st] += 1
                o["sig"] = ((st,), cnt[st])
            else:
                o["sig"] = None

        def semof(key):
            return dma_sems[key[1]][key[2]] if key[0] == "d" else sems[key[0]]

        known = {s: {} for s in self.STREAMS}
        for o in self.ops:
            kn = known[o["stream"]]
            waits = []
            if o["dma"]:
                key, val = o["sig"]
                if o["prev_val"] > kn.get(key, 0):
                    waits.append((key, o["prev_val"]))
                    kn[key] = o["prev_val"]
            for d in sorted(o["deps"]):
                key, val = self.ops[d]["sig"]
                if kn.get(key, 0) >= val:
                    continue
                waits.append((key, val))
                kn[key] = val
                for k2, v2 in self.ops[d]["snap"].items():
                    if kn.get(k2, 0) < v2:
                        kn[k2] = v2
            best = {}
            for k, v in waits:
                best[k] = max(best.get(k, 0), v)
            o["waits"] = list(best.items())
            snap = dict(kn)
            if o["sig"] is not None and not o["dma"]:
                snap[o["sig"][0]] = max(snap.get(o["sig"][0], 0), o["sig"][1])
            o["snap"] = snap
        self.stats = {s: sum(1 for o in self.ops if o["stream"] == s) for s in self.STREAMS}
        self.nwaits = sum(len(o["waits"]) for o in self.ops)
        engmap = {"pe": "tensor", "act": "scalar", "dve": "vector", "pool": "gpsimd", "sp": "sync"}
        with stack:
            with nc.Block() as block:
                for s in self.STREAMS:
                    myops = [o for o in self.ops if o["stream"] == s]
                    if not myops:
                        continue

                    def body(eng, myops=myops):
                        for o in myops:
                            for (key, val) in o["waits"]:
                                eng.wait_ge(semof(key), val)
                            ins = o["fn"](eng)
                            if o["sig"] is not None:
                                ins.then_inc(semof(o["sig"][0]), 16 if o["dma"] else 1)
                    getattr(block, engmap[s])(body)


def fsz(t):
    return int(np.prod(list(t.shape)[1:]))


def A(t, foff, dims, p0=0, npart=128):
    F = fsz(t)
    return bass.AP(t, p0 * F + foff, [[F, npart]] + [list(d) for d in dims])


def build(debug=False):
    stop = int(os.environ.get('KSTOP', '9'))
    nc = bass.Bass("TRN2", target_bir_lowering=False)
    P = Prog(nc)
    xT_d = P.dram("xT", [128, KC, TT], F32, kind="ExternalInput")
    xs_d = P.dram("xs", [T, D], F32, kind="ExternalInput")
    cT_d = P.dram("cT", [128, KC], F32, kind="ExternalInput")
    wada_d = P.dram("wada", [NSA, 128, KC, 256], F32, kind="ExternalInput")
    NCP = 48 + 248 + 8 * 4 + 2
    cp_d = P.dram("cpack", [128, NCP], F32, kind="ExternalInput")
    wsl_d = P.dram("wslabs", [len(WSLABS), 128, KC, 256], F32, kind="ExternalInput")
    oh_d = P.dram("onehot", [32, 128], F32, kind="ExternalInput")
    rb_d = P.dram("rel_bias", [32, 16], F32, kind="ExternalInput")
    sk_d = P.dram("sinks", [16], F32, kind="ExternalInput")
    wpw_d = P.dram("wpw", [128, 8, 1024], F32, kind="ExternalInput")
    wout_d = P.dram("wout", [128, KC, D], F32, kind="ExternalInput")
    lng_d = P.dram("ln_g", [D], F32, kind="ExternalInput")
    lnb_d = P.dram("ln_b", [D], F32, kind="ExternalInput")
    id_d = P.dram("ident", [128, 128], F32, kind="ExternalInput")
    out_d = P.dram("out", [T, D], F32, kind="ExternalOutput")
    LB = 384
    bsc_d = P.dram("bias_scratch", [16 * 128 * LB + 512], F32)
    gsc_d = P.dram("gate_scratch", [D], F32)

    B0 = 16640
    LIMIT = 229376
    o_const = B0
    o_bias = o_const + 4096
    o_vT = o_bias + 16384
    o_sgc = o_vT + 32768
    o_hT = o_sgc + 16384
    o_wb = o_hT + 36864
    o_qT = o_wb + NWB * 8192
    o_kT = o_qT + 16384
    o_V1 = o_kT + 4608
    o_sga = o_V1 + 4768
    o_uT = o_sga + 16384
    o_sig = o_uT + NU * UW * 4
    o_att = o_sig + 4096
    o_end = o_att + 15360 + 512 + 2048
    assert o_end <= LIMIT, o_end

    _oc = [o_const]

    def cst(name, shape, dtype):
        nbytes = int(np.prod(shape[1:])) * esize(dtype)
        t_ = P.sb(name, shape, dtype, _oc[0])
        _oc[0] += (nbytes + 31) // 32 * 32
        return t_
    cpk = cst("cpk", [128, NCP], F32)
    modT = cst("modT", [128, 48], F32)
    s1T = cst("s1T", [128, 16], F32)
    esink = cst("esink", [128, 16], F32)
    identb = cst("identb", [128, 128], BF16)
    onesb = cst("onesb", [128, 128], BF16)
    den = cst("den", [128, 8], F32)
    rden = cst("rden", [128, 8], F32)
    bnst = cst("bnst", [128, 3, 24], F32)
    mv = cst("mv", [128, 3, 2], F32)
    sd = cst("sd", [128, 3, 2], F32)
    rbt = cst("rbt", [32, 16], F32)
    oht = cst("oht", [32, 128], F32)
    ct = cst("ct", [128, KC], F32)
    ctf = cst("ctf", [128, KC], F32)
    chl = cst("chl", [128, KC, 2], BF16)
    ones2 = cst("ones2", [2, 2], F32)
    assert _oc[0] <= o_const + 4096, _oc[0]
    C_BADA, C_CW, C_CB, C_LG, C_LB, C_BPW, C_FLAG = 0, 48, 296, 304, 312, 320, 328

    bhi = P.sb("bhi", [128, 2, 2048], BF16, o_bias)
    blo = P.sb("blo", [128, 2, 2048], BF16, o_bias + 8192)
    bias32 = P.sb("bias32", [128, 2, 2048], F32, o_sga)
    vT = P.sb("vT", [128, 8, T], F32, o_vT)
    yTa = P.sb("yTa", [128, 8, T], BF16, o_sgc)
    hT = P.sb("hT", [128, KC, TT], BF16, o_hT)
    yTc = P.sb("yTc", [128, 8, T], BF16, o_hT)
    wb = [P.sb("wb%d" % i, [128, KC, 256], BF16, o_wb + i * 8192) for i in range(NWB)]
    qT = P.sb("qT", [128, 8, T], BF16, o_qT)
    sgcT = P.sb("sgcT", [128, 8, T], BF16, o_qT)
    kT = P.sb("kT", [128, 2, TT], BF16, o_kT)
    V1 = P.sb("V1", [128, 9, 4, 66], BF16, o_V1)
    sga = P.sb("sga", [128, 8, 1024], BF16, o_sga)
    uT = [P.sb("uT%d" % i, [128, UW], F32, o_uT + i * UW * 4) for i in range(NU)]
    sig = [P.sb("sig%d" % i, [128, 512], F32, o_sig + i * 2048) for i in range(2)]
    NWA = 4
    wa = [P.sb("wa%d" % i, [128, KC, 256], BF16, o_qT + i * 8192) for i in range(NWA)]
    modrow = P.sb("modrow", [2, 4096], F32, o_qT + 32768)
    Eb = P.sb("Eb", [16, LB], F32, o_hT + 15 * TT * 2)
    xoff = [o_vT + i * 4608 for i in range(7)] + [o_sgc + i * 4608 for i in range(3)] + \
           [o_uT + 8544 + i * 4608 for i in range(6)]
    assert o_qT + 24576 + 24576 + LB * 4 <= o_uT + 8544 and o_uT + 8544 + 6 * 4608 <= o_end
    xst = [P.sb("xst%d" % i, [128, TT], F32, xoff[i]) for i in range(KC)]
    modrow_g = P.sb("modrow_g", [2, 2048], F32, o_att)
    PTt = [P.sb("PTt%d" % i, [128, 2, 512], BF16, o_att + 8192 + i * 2048) for i in range(2)] + \
          [P.sb("PTt2", [128, 2, 512], BF16, o_att + 15360 + 512)]
    otmp = [P.sb("otmp%d" % i, [128, 256], F32, o_att + 12288 + i * 1024) for i in range(2)]
    yatm = [P.sb("yatm%d" % i, [128, 256], BF16, o_att + 14336 + i * 512) for i in range(2)] + \
           [P.sb("yatm2", [128, 256], BF16, o_att + 15360)]
    zT = P.sb("zT", [128, 8, T], BF16, o_bias)
    wpwb = P.sb("wpwb", [128, 8, 1024], BF16, o_hT + 16384)
    assert 32768 <= 36864
    o_ln = o_att
    vbb = [P.sb("vbb%d" % i, [128, 512], BF16, o_ln + i * 1024) for i in range(2)]
    vsq = [P.sb("vsq%d" % i, [128, 512], BF16, o_ln + 2048 + i * 1024) for i in range(2)]
    mus = [P.sb("mu_sb%d" % i, [128, 512], F32, o_ln + 4096 + i * 2048) for i in range(2)]
    rss = [P.sb("rs_sb%d" % i, [128, 512], F32, o_sig + i * 2048) for i in range(2)]
    tnorm = [P.sb("tnorm%d" % i, [128, 512], F32, o_ln + 8192 + i * 2048) for i in range(3)]
    assert o_ln + 14336 <= o_att + 15360
    woutb_a = P.sb("woutb_a", [128, 6, D], BF16, o_wb)
    woutb_b = P.sb("woutb_b", [128, 10, D], BF16, o_kT)
    assert o_kT + 40960 <= o_sig

    def wout_kc(kc):
        return woutb_a[:, kc, :] if kc < 6 else woutb_b[:, kc - 6, :]

    def yT_kc(kc):
        return yTa[:, kc, :] if kc < 8 else yTc[:, kc - 8, :]
    gate_bc = P.sb("gate_bc", [128, D], F32, o_vT)
    lng_bc = P.sb("lng_bc", [128, D], F32, o_vT + 8192)
    lnb_bc = P.sb("lnb_bc", [128, D], F32, o_vT + 16384)
    xt4 = [P.sb("xt4_%d" % i, [128, D], F32, o_bias + i * 8192) for i in range(2)] + \
          [P.sb("xt4_2", [128, D], F32, o_vT + 24576)]
    r4 = [P.sb("r4_%d" % i, [128, D], F32, o_hT + 16384 + i * 8192) for i in range(2)] + \
         [P.sb("r4_2", [128, D], F32, o_uT + 15232)]
    o4 = [P.sb("o4_%d" % i, [128, D], F32, o_qT + i * 8192) for i in range(2)] + \
         [P.sb("o4_2", [128, D], F32, o_uT + 15232 + 8192)]
    assert o_kT + 40960 <= o_uT + 15232 and o_uT + 15232 + 16384 <= o_end

    psf = P.psum("psf", [128, 3584], F32)
    psb = P.psum("psb", [128, 1024], BF16)

    def dma(stream, out, in_, **kw):
        P.op(stream, lambda e: e.dma_start(out=out, in_=in_, **kw), reads=[in_], writes=[out], dma=True)

    def act(out, in_, func, bias=None, scale=None, accum_out=None):
        rd = [in_]
        kw = {}
        if accum_out is not None:
            kw["accum_out"] = accum_out
        if bias is not None:
            kw["bias"] = bias
            if not isinstance(bias, float):
                rd.append(bias)
        if scale is not None:
            kw["scale"] = scale
            if not isinstance(scale, float):
                rd.append(scale)
        P.op("act", lambda e: e.activation(out=out, in_=in_, func=func, **kw), reads=rd,
             writes=[out] + ([accum_out] if accum_out is not None else []))

    def tt(eng, out, a, b, op):
        P.op(eng, lambda e: e.tensor_tensor(out=out, in0=a, in1=b, op=op), reads=[a, b], writes=[out])

    def ts(eng, out, in0, s1, s2, op0, op1=None):
        rd = [in0] + [s for s in (s1, s2) if s is not None and not isinstance(s, float)]
        if op1 is None:
            P.op(eng, lambda e: e.tensor_scalar(out=out, in0=in0, scalar1=s1, scalar2=None, op0=op0),
                 reads=rd, writes=[out])
        else:
            P.op(eng, lambda e: e.tensor_scalar(out=out, in0=in0, scalar1=s1, scalar2=s2, op0=op0, op1=op1),
                 reads=rd, writes=[out])

    def stt(out, in0, scalar, in1, op0, op1):
        rd = [in0, in1] + ([] if isinstance(scalar, float) else [scalar])
        P.op("dve", lambda e: e.scalar_tensor_tensor(out=out, in0=in0, scalar=scalar, in1=in1,
                                                     op0=op0, op1=op1), reads=rd, writes=[out])

    def cpy(eng, out, in_):
        P.op(eng, lambda e: e.tensor_copy(out=out, in_=in_), reads=[in_], writes=[out])

    def mm_groups(groups, open_=True, close=True):
        rd, wr = [], []
        for out, pairs in groups:
            wr.append(out)
            for l, r in pairs:
                rd += [l, r]

        def fn(e):
            ins = None
            for out, pairs in groups:
                n = len(pairs)
                for i, (l, r) in enumerate(pairs):
                    ins = e.matmul(out, lhsT=l, rhs=r, start=(open_ and i == 0), stop=(close and i == n - 1),
                                   skip_group_check=not (open_ and close))
            return ins
        P.op("pe", fn, reads=rd, writes=wr)

    bankctr = [0]

    def next_bank():
        b = bankctr[0] % 4
        bankctr[0] += 1
        return b * 512

    dma("sp", cpk[:, :], cp_d.ap())
    dma("sp", ct[:, :], cT_d.ap())
    dma("sp", esink[:, :], bass.AP(sk_d, 0, [[0, 128], [1, 16]]))
    dma("sp", rbt[:, :], rb_d.ap())
    dma("sp", oht[:, :], oh_d.ap())
    dma("pool", identb[:, :], id_d.ap())
    assert SLABS[0] == ("G", 0)
    NPRE = NWA

    def xT_load(kc):
        dma("pool", xst[kc][:, :], bass.AP(xT_d, kc * TT, [[KC * TT, 128], [1, TT]]))
    for j in range(min(NPRE, NSA0)):
        dma("pool", wa[j % NWA][:, :, :], wada_d.ap()[j])
        xT_load(j)
    dma("pool", wb[0][:, :, :], wsl_d.ap()[0])
    act(ct[:, :], ct[:, :], AF.Silu)
    ts("dve", cpk[:, C_CW:C_CW + 248], cpk[:, C_CW:C_CW + 248], 0.5, None, ALU.mult)
    act(esink[:, :], esink[:, :], AF.Exp)
    P.op("dve", lambda e: e.memset(onesb[:, :], 1.0), writes=[onesb[:, :]])
    P.op("dve", lambda e: e.memset(ones2[:, :], 1.0), writes=[ones2[:, :]])
    P.op("dve", lambda e: e.memset(Eb[:, :], NEG), writes=[Eb[:, :]])
    cpy("dve", A(chl, 0, [[2, KC]]), ct[:, :])
    cpy("dve", ctf[:, :], A(chl, 0, [[2, KC]]))
    tt("dve", A(chl, 1, [[2, KC]]), ct[:, :], ctf[:, :], ALU.subtract)

    mm_groups([(A(psf, 4 * 512, [[1, 128]], 0, 16), [(rbt[:, :], oht[:, :])])])
    act(A(Eb, 128, [[1, 128]], 0, 16), A(psf, 4 * 512, [[1, 128]], 0, 16), AF.Identity)
    dma("sp", bass.AP(bsc_d, 0, [[128 * LB, 16], [LB, 128], [1, LB]]),
        bass.AP(Eb, 0, [[LB, 16], [0, 128], [1, LB]]))

    MOD_BANK = 6 * 512
    MT = MOD_BANK + 256
    EARLY = [(0, HALO, 512, 0), (1, HALO, 512, 512), (0, HALO + 512, 512, 1024), (1, HALO + 512, 512, 1536),
             (0, HALO - 32, 32, 2048), (1, HALO - 32, 32, 2560)]
    def emit_early(j):
        def early(e, j=j):
            ins = None
            for (ci, lo, n, col) in EARLY:
                ins = e.matmul(psf[:, col:col + n], lhsT=wb[0][:, j, ci * 128:(ci + 1) * 128], rhs=hT[:, j, lo:lo + n],
                               start=(j == 0), stop=(j == NSA0 - 1), skip_group_check=True)
            return ins
        P.op("pe", early, reads=[wb[0][:, j, :], hT[:, j, :]], writes=[psf[:, col:col + n] for (_, _, n, col) in EARLY])

    for j in range(NSA0):
        w_ = wa[j % NWA]
        bk = MOD_BANK
        mm_groups([(A(psf, bk, [[1, 256]], 0, 2), [(chl[:, kc, :], w_[:, kc, :]) for kc in range(KC)])])
        act(A(modrow, j * 256, [[1, 256]], 0, 2), A(psf, bk, [[1, 256]], 0, 2), AF.Identity)
        if j + NPRE < NSA0:
            dma("pool", wa[(j + NPRE) % NWA][:, :, :], wada_d.ap()[j + NPRE])
            xT_load(j + NPRE)
        def modtr(e, j=j):
            ins = None
            for h2 in range(2):
                ins = e.matmul(A(psf, MT + 2 * j + h2, [[1, 1]]),
                               lhsT=A(modrow, j * 256 + h2 * 128, [[1, 128]], 0, 2),
                               rhs=A(ones2, 0, [[1, 1]], 0, 2), start=True, stop=True)
            return ins
        P.op("pe", modtr, reads=[A(modrow, j * 256, [[1, 256]], 0, 2), ones2[:, :]],
             writes=[A(psf, MT + 2 * j, [[1, 2]])])
        if j >= 1:
            emit_early(j - 1)
        tt("dve", modT[:, j:j + 1], A(psf, MT + 2 * j, [[1, 1]]), cpk[:, C_BADA + j:C_BADA + j + 1], ALU.add)
        ts("dve", s1T[:, j:j + 1], A(psf, MT + 2 * j + 1, [[1, 1]]), cpk[:, C_BADA + 16 + j:C_BADA + 17 + j], 1.0,
           ALU.add, ALU.add)
        if j % 2 == 0:
            act(hT[:, j, :], xst[j][:, :], AF.Identity, bias=modT[:, j:j + 1], scale=s1T[:, j:j + 1])
        else:
            ts("dve", hT[:, j, :], xst[j][:, :], s1T[:, j:j + 1], modT[:, j:j + 1], ALU.mult, ALU.add)

    emit_early(NSA0 - 1)
    for pc in range(2):
        src = bass.AP(bsc_d, 256 - 128 * pc, [[LB - 1, 128], [128 * LB, 16], [1, 128]])
        dma("sp", A(bias32, pc * 2048, [[128, 16], [1, 128]]), src)
    P.op("dve", lambda e: e.memset(A(V1, 0, [[1, 9 * 4 * 66]]), 1.0), writes=[A(V1, 0, [[1, 9 * 4 * 66]])])

    conv_q = []

    def queue_conv(r, ub):
        acc = vT[:, r, :]
        act(acc, ub[:, 2:2 + T], AF.Identity, bias=cpk[:, C_CB + r:C_CB + r + 1],
            scale=cpk[:, C_CW + r * 31:C_CW + r * 31 + 1])
        for j in range(1, 31):
            conv_q.append(lambda j=j: stt(acc, ub[:, 2 + j:2 + j + T],
                                          cpk[:, C_CW + r * 31 + j:C_CW + r * 31 + j + 1], acc, ALU.mult, ALU.add))

    def drain_conv(n):
        for _ in range(min(n, len(conv_q))):
            conv_q.pop(0)()

    flag = cpk[:, C_FLAG:C_FLAG + 1]

    iters = [(g, b) for g in range(4) for b in range(8)]
    att_ready = set()
    S_BANK, O_BANK = 4 * 512, 6 * 512
    st_att = dict(s=0, pv=[], tr=[])

    def att_S(i):
        g, b = iters[i]
        m, s = g // 2, g % 2
        p0 = s * 64
        qap = A(qT, (4 * m) * T + b * 128, [[T, 4], [1, 128]], p0, 64)
        kprev = A(kT, m * TT + b * 128, [[1, 128]], p0, 64)
        kcur = A(kT, m * TT + (b + 1) * 128, [[1, 128]], p0, 64)
        o_p = A(psf, S_BANK, [[128, 4], [1, 128]])
        o_c = A(psf, S_BANK + 512, [[128, 4], [1, 128]])
        bp = lambda t_, pc: A(t_, pc * 2048 + g * 512, [[128, 4], [1, 128]])
        mm_groups([(o_p, [(identb[:, :], bp(bhi, 0)), (identb[:, :], bp(blo, 0))]),
                   (o_c, [(identb[:, :], bp(bhi, 1)), (identb[:, :], bp(blo, 1))])], close=False)
        mm_groups([(o_p, [(kprev, qap)]), (o_c, [(kcur, qap)])], open_=False)
        pt_ = PTt[i % 3]
        if b == 0:
            act(pt_[:, 0, :], psf[:, S_BANK:S_BANK + 512], AF.Exp, bias=cpk[:, C_FLAG + 1:C_FLAG + 2])
        else:
            act(pt_[:, 0, :], psf[:, S_BANK:S_BANK + 512], AF.Exp)
        act(pt_[:, 1, :], psf[:, S_BANK + 512:S_BANK + 1024], AF.Exp)

    def att_PV(i):
        g, b = iters[i]
        pt_ = PTt[i % 3]
        groups = []
        for j in range(4):
            groups.append((psf[:, O_BANK + j * 128:O_BANK + j * 128 + 65],
                           [(pt_[:, 0, j * 128:(j + 1) * 128], A(V1, b * 264 + g * 66, [[1, 65]])),
                            (pt_[:, 1, j * 128:(j + 1) * 128], A(V1, (b + 1) * 264 + g * 66, [[1, 65]]))]))
        mm_groups(groups)
        dn = A(den, (i % 2) * 4, [[1, 4]])
        rdn = A(rden, (i % 2) * 4, [[1, 4]])
        tt("dve", dn, A(psf, O_BANK + 64, [[128, 4]]), esink[:, 4 * g:4 * g + 4], ALU.add)
        P.op("dve", lambda e: e.reciprocal(out=rdn, in_=dn), reads=[dn], writes=[rdn])
        ot = otmp[i % 2]
        tt("dve", A(ot, 0, [[64, 4], [1, 64]]), A(psf, O_BANK, [[128, 4], [1, 64]]),
           A(rden, (i % 2) * 4, [[1, 4], [0, 64]]), ALU.mult)
        tt("dve", yatm[i % 3][:, :], ot[:, :], sga[:, b, g * 256:(g + 1) * 256], ALU.mult)

    def att_T(i):
        g, b = iters[i]
        ya = yatm[i % 3]
        off = (i % 2) * 256

        def tr(e):
            ins = None
            for c in range(2):
                ins = e.transpose(out=psb[:, off + c * 128:off + (c + 1) * 128], in_=ya[:, c * 128:(c + 1) * 128],
                                  identity=identb[:, :])
            return ins
        P.op("pe", tr, reads=[ya[:, :], identb[:, :]], writes=[psb[:, off:off + 256]])
        act(A(yTa, (2 * g) * T + b * 128, [[T, 2], [1, 128]]), A(psb, off, [[128, 2], [1, 128]]), AF.Identity)

    att_on = [True]

    def att_advance(allow_s=True):
        if stop < 2:
            return False
        did = False
        i_ = st_att["s"]
        can_s = allow_s and i_ < len(iters) and iters[i_][0] in att_ready
        if len(st_att["tr"]) > 1 or (st_att["tr"] and not st_att["pv"] and not can_s):
            att_T(st_att["tr"].pop(0)); did = True
        if len(st_att["pv"]) > 1 or (st_att["pv"] and not can_s):
            j = st_att["pv"].pop(0)
            att_PV(j); st_att["tr"].append(j); did = True
        if can_s:
            att_S(i_); st_att["pv"].append(i_); st_att["s"] += 1; did = True
        return did

    def unitT(w, ci, lo, n):
        bk = next_bank()
        out = psf[:, bk:bk + n]
        mm_groups([(out, [(w[:, kc, ci * 128:(ci + 1) * 128], hT[:, kc, lo:lo + n]) for kc in range(KC)])])
        return out

    credit = [0.0]

    conv_share = [CONV_DVE_SHARE]

    def after_unit(us=4.0):
        if us >= 2.0:
            att_advance(allow_s=att_on[0])
        credit[0] = min(credit[0] + us * conv_share[0], 12.0)
        while credit[0] >= 1.22 and conv_q:
            drain_conv(1)
            credit[0] -= 1.22

    widx = {}
    for s_ in SLABS:
        if s_[0] != "M":
            widx[s_] = len(widx)

    def slab_dma(si):
        if si < len(SLABS):
            typ_, idx_ = SLABS[si]
            src = wada_d.ap()[NSA0 + idx_] if typ_ == "M" else wsl_d.ap()[widx[(typ_, idx_)]]
            dma("pool", wb[si % NWB][:, :, :], src)

    def glu_evac(pa, pb, ub, n, c0):
        sg = sig[(bankctr[0] // 2) % 2]
        bankctr[0] += 2
        act(sg[:, 0:n], pb, AF.Tanh, scale=0.5)
        stt(ub[:, c0:c0 + n], sg[:, 0:n], 1.0, pa, ALU.add, ALU.mult)
        if n == 32:
            ts("dve", ub[:, 0:32], ub[:, 0:32], flag, None, ALU.mult)

    for (lo, n, c0, col) in ((HALO, 512, 32, 0), (HALO + 512, 512, 32 + 512, 1024), (HALO - 32, 32, 0, 2048)):
        glu_evac(psf[:, col:col + n], psf[:, col + 512:col + 512 + n], uT[0], n, c0)
    queue_conv(0, uT[0])
    for si in range(1, min(NWB - 1, len(SLABS))):
        slab_dma(si)
    vslot = [0]
    for si, (typ, idx) in enumerate(SLABS):
        w = wb[si % NWB]
        slab_dma(si + NWB - 1)
        if si == 0:
            continue
        if si == 3:
            for pc in range(2):
                cpy("dve", bhi[:, pc, :], bias32[:, pc, :])
                tt("dve", blo[:, pc, :], bias32[:, pc, :], bhi[:, pc, :], ALU.subtract)
        att_on[0] = (typ != "A") and not (typ == "C" and idx >= 2)
        conv_share[0] = 1.05 if typ == "C" else (0.95 if typ in ("Q", "K", "V", "A") else CONV_DVE_SHARE)
        if typ == "M":
            bk = next_bank()
            mm_groups([(A(psf, bk, [[1, 256]], 0, 2), [(chl[:, kc, :], w[:, kc, :]) for kc in range(KC)])])
            act(A(modrow_g, idx * 256, [[1, 256]], 0, 2), A(psf, bk, [[1, 256]], 0, 2), AF.Identity)
            after_unit(2.0)
            if idx == 7:
                bk = next_bank()

                def gtr(e, bk=bk):
                    ins = None
                    for jc in range(16):
                        ins = e.matmul(A(psf, bk + jc, [[1, 1]]), lhsT=A(modrow_g, jc * 128, [[1, 128]], 0, 2),
                                       rhs=A(ones2, 0, [[1, 1]], 0, 2), start=True, stop=True)
                    return ins
                P.op("pe", gtr, reads=[A(modrow_g, 0, [[1, 2048]], 0, 2), ones2[:, :]], writes=[A(psf, bk, [[1, 16]])])
                tt("dve", modT[:, 32:48], A(psf, bk, [[1, 16]]), cpk[:, C_BADA + 32:C_BADA + 48], ALU.add)
                dma("sp", bass.AP(gsc_d, 0, [[1, 128], [128, 16]]), modT[:, 32:48], allow_slow_non_contiguous=True)
        elif typ == "G":
            r = idx
            ub = uT[r % NU]
            for gi, (lo, n, c0) in enumerate(((HALO - 32, 32, 0), (HALO, 512, 32), (HALO + 512, 512, 32 + 512))):
                pa = unitT(w, 0, lo, n)
                after_unit(4.0 if n == 512 else 0.6)
                pb = unitT(w, 1, lo, n)
                sg = sig[(bankctr[0] // 2) % 2]
                act(sg[:, 0:n], pb, AF.Tanh, scale=0.5)
                stt(ub[:, c0:c0 + n], sg[:, 0:n], 1.0, pa, ALU.add, ALU.mult)
                if n == 32:
                    ts("dve", ub[:, 0:32], ub[:, 0:32], flag, None, ALU.mult)
                if gi == 2:
                    queue_conv(r, ub)
                after_unit(4.0 if n == 512 else 0.6)
        elif typ in ("Q", "K", "C"):
            for ci in range(2):
                ch = idx * 2 + ci
                for tg in range(2):
                    po = unitT(w, ci, HALO + tg * 512, 512)
                    cs = slice(tg * 512, (tg + 1) * 512)
                    if typ == "Q":
                        act(qT[:, ch, cs], po, AF.Identity, scale=0.125)
                    elif typ == "K":
                        act(kT[:, ci, HALO + tg * 512:HALO + (tg + 1) * 512], po, AF.Identity)
                    else:
                        act(sgcT[:, ch, cs], po, AF.Silu)
                    after_unit()
                if typ == "K":
                    po = unitT(w, ci, 0, HALO)
                    act(kT[:, ci, 0:HALO], po, AF.Identity)
                    after_unit(1.1)
        else:
            tiles = range(9) if typ == "V" else range(1, 9)
            for tt_i in tiles:
                bk = next_bank()
                pt = psf[:, bk:bk + 256]
                mm_groups([(pt, [(hT[:, kc, tt_i * 128:(tt_i + 1) * 128], w[:, kc, :]) for kc in range(KC)])])
                if typ == "V":
                    act(A(V1, tt_i * 264, [[66, 4], [1, 64]]), A(psf, bk, [[64, 4], [1, 64]]), AF.Identity)
                else:
                    act(sga[:, tt_i - 1, idx * 256:(idx + 1) * 256], pt, AF.Silu)
                after_unit(2.1)
            if typ == "A":
                att_ready.add(idx)
    att_on[0] = True
    while att_advance(True):
        drain_conv(CONV_PER_UNIT)
    drain_conv(10 ** 6)

    if stop >= 4:
        for kc in list(range(6, 12)):
            dma("pool", wout_kc(kc), wout_d.ap()[:, kc, :])
    if stop >= 3:
        for half in range(2):
            dma("pool", wpwb[:, half * 4:(half + 1) * 4, :], wpw_d.ap()[:, half * 4:(half + 1) * 4, :])
    if stop >= 4:
        for kc in list(range(12, 16)) + list(range(6)):
            dma("pool", wout_kc(kc), wout_d.ap()[:, kc, :])
    def t3_prep_stats_all():
        for c in range(8):
            for tg in range(2):
                tsl = slice(tg * 512, (tg + 1) * 512)
                pm = psf[:, tg * 1024:tg * 1024 + 512]
                pq = psf[:, tg * 1024 + 512:tg * 1024 + 1024]
                k_ = (c * 2 + tg) % 2
                act(vbb[k_][:, :], vT[:, c, tsl], AF.Identity)
                act(vsq[k_][:, :], vT[:, c, tsl], AF.Square)
                P.op("pe", lambda e, c=c, pm=pm, k_=k_: e.matmul(pm, lhsT=onesb[:, :], rhs=vbb[k_][:, :],
                                                                 start=(c == 0), stop=(c == 7), skip_group_check=True),
                     reads=[onesb[:, :], vbb[k_][:, :]], writes=[pm])
                P.op("pe", lambda e, c=c, pq=pq, k_=k_: e.matmul(pq, lhsT=onesb[:, :], rhs=vsq[k_][:, :],
                                                                 start=(c == 0), stop=(c == 7), skip_group_check=True),
                     reads=[onesb[:, :], vsq[k_][:, :]], writes=[pq])

    def t3_small(tg):
        pm = psf[:, tg * 1024:tg * 1024 + 512]
        pq = psf[:, tg * 1024 + 512:tg * 1024 + 1024]
        mu_, rs_ = mus[tg], rss[tg]
        act(mu_[:, :], pm, AF.Identity, scale=1.0 / 1024.0)
        stt(rs_[:, :], mu_[:, :], -1.0, mu_[:, :], ALU.mult, ALU.mult)
        stt(rs_[:, :], pq, 1.0 / 1024.0, rs_[:, :], ALU.mult, ALU.add)
        act(rs_[:, :], rs_[:, :], AF.Ln, bias=EPS)
        act(rs_[:, :], rs_[:, :], AF.Exp, scale=-0.5)

    def t3_norm(tg):
        tsl = slice(tg * 512, (tg + 1) * 512)
        mu_, rs_ = mus[tg], rss[tg]
        for c in range(8):
            eng = "pool" if c == 7 else "dve"
            tn = tnorm[c % 3]
            tt(eng, tn[:, :], vT[:, c, tsl], mu_[:, :], ALU.subtract)
            tt(eng, tn[:, :], tn[:, :], rs_[:, :], ALU.mult)
            act(zT[:, c, tsl], tn[:, :], AF.Silu, bias=cpk[:, C_LB + c:C_LB + c + 1], scale=cpk[:, C_LG + c:C_LG + c + 1])

    def t3_pw(tg):
        tsl = slice(tg * 512, (tg + 1) * 512)
        for f in range(8):
            k_ = (tg * 8 + f) % 2
            pp = psf[:, 2048 + k_ * 512:2048 + (k_ + 1) * 512]
            mm_groups([(pp, [(wpwb[:, kc, f * 128:(f + 1) * 128], zT[:, kc, tsl]) for kc in range(8)])])
            stt(yTc[:, f, tsl], pp, cpk[:, C_BPW + f:C_BPW + f + 1], sgcT[:, f, tsl], ALU.add, ALU.mult)

    if stop >= 3:
        t3_prep_stats_all()
        t3_small(0)
        t3_small(1)
        t3_norm(0)
        t3_norm(1)
        t3_pw(0)
        t3_pw(1)

    if stop >= 4:
        dma("sp", gate_bc[:, :], bass.AP(gsc_d, 0, [[0, 128], [1, D]]))
        dma("sp", lng_bc[:, :], bass.AP(lng_d, 0, [[0, 128], [1, D]]))
        dma("sp", lnb_bc[:, :], bass.AP(lnb_d, 0, [[0, 128], [1, D]]))
        for t8 in range(3):
            dma("sp", xt4[t8][:, :], xs_d.ap()[t8 * 128:(t8 + 1) * 128, :])
    hctr = [0]

    def t4_A(t8):
        xt_, r_ = xt4[t8 % 3], r4[t8 % 3]
        act(xt_[:, :], xt_[:, :], AF.Identity, scale=float(ALPHA))
        for hf in range(2):
            base = (hctr[0] % 3) * 1024
            hctr[0] += 1
            groups = []
            for n2 in range(2):
                ng = hf * 2 + n2
                groups.append((psf[:, base + n2 * 512:base + (n2 + 1) * 512],
                               [(yT_kc(kc)[:, t8 * 128:(t8 + 1) * 128], wout_kc(kc)[:, ng * 512:(ng + 1) * 512])
                                for kc in range(KC)]))
            mm_groups(groups)
            hs = slice(hf * 1024, (hf + 1) * 1024)
            for n2 in range(2):
                cs = slice(hf * 1024 + n2 * 512, hf * 1024 + (n2 + 1) * 512)
                tt("dve", r_[:, cs], psf[:, base + n2 * 512:base + (n2 + 1) * 512], gate_bc[:, cs], ALU.mult)
            tt("dve", r_[:, hs], r_[:, hs], xt_[:, hs], ALU.add)
            for q4 in (2 * hf, 2 * hf + 1):
                P.op("dve", lambda e, q4=q4, r_=r_, i2=t8 % 3: e.bn_stats(out=bnst[:, i2, q4 * 6:(q4 + 1) * 6],
                                                                         in_=r_[:, q4 * 512:(q4 + 1) * 512]),
                     reads=[r_[:, q4 * 512:(q4 + 1) * 512]], writes=[bnst[:, t8 % 3, q4 * 6:(q4 + 1) * 6]])
        if t8 + 3 < 8:
            dma("sp", xt4[t8 % 3][:, :], xs_d.ap()[(t8 + 3) * 128:(t8 + 4) * 128, :])
        i2 = t8 % 3
        P.op("dve", lambda e, i2=i2: e.bn_aggr(out=mv[:, i2, :], in_=bnst[:, i2, :]),
             reads=[bnst[:, i2, :]], writes=[mv[:, i2, :]])
        act(sd[:, i2, 0:1], mv[:, i2, 1:2], AF.Sqrt, bias=EPS)
        P.op("dve", lambda e, i2=i2: e.reciprocal(out=sd[:, i2, 0:1], in_=sd[:, i2, 0:1]),
             reads=[sd[:, i2, 0:1]], writes=[sd[:, i2, 0:1]])
        stt(sd[:, i2, 1:2], mv[:, i2, 0:1], -1.0, sd[:, i2, 0:1], ALU.mult, ALU.mult)

    def t4_B(t8):
        r_, o_ = r4[t8 % 3], o4[t8 % 3]
        i2 = t8 % 3
        act(o_[:, :], r_[:, :], AF.Identity, bias=sd[:, i2, 1:2], scale=sd[:, i2, 0:1])
        PS = 384 if t8 < 7 else 256
        tt("pool", o_[:, 0:PS], o_[:, 0:PS], lng_bc[:, 0:PS], ALU.mult)
        tt("dve", o_[:, PS:D], o_[:, PS:D], lng_bc[:, PS:D], ALU.mult)
        tt("pool", o_[:, 0:PS], o_[:, 0:PS], lnb_bc[:, 0:PS], ALU.add)
        tt("dve", o_[:, PS:D], o_[:, PS:D], lnb_bc[:, PS:D], ALU.add)
        if t8 < 7:
            dma("pool", out_d.ap()[t8 * 128:(t8 + 1) * 128, :], o_[:, :])
        else:
            dma("sp", out_d.ap()[t8 * 128:(t8 + 1) * 128, PS:D], o_[:, PS:D])
            dma("pool", out_d.ap()[t8 * 128:(t8 + 1) * 128, 0:PS], o_[:, 0:PS])

    if stop >= 4:
        t4_A(0)
        for t8 in range(1, 8):
            t4_A(t8)
            t4_B(t8 - 1)
        t4_B(7)

    dbg = []
    if debug:
        def dump(name, t):
            shp = list(t.shape)
            d_ = P.dram("dbg_" + name, shp, t.dtype, kind="ExternalOutput")
            src = A(t, 0, [[1, fsz(t)]], 0, shp[0])
            dst = bass.AP(d_, 0, [[fsz(t), shp[0]], [1, fsz(t)]])
            dma("sp", dst, src)
            dbg.append(d_)
        for nm, t in debug_targets(locals()):
            dump(nm, t)
    P.op("sp", lambda e: e.nop(), reads=[out_d.ap()] + [d_.ap() for d_ in dbg])
    P.emit()
    return nc, P


def debug_targets(loc):
    names = os.environ.get("KDEBUG", "").split(",")
    return [(n, loc[n]) for n in names if n and n in loc]


def t5_bucket_np(d):
    d = np.asarray(d)
    dd = np.maximum(d, 1).astype(np.float32)
    large = 16 + (np.log(dd / np.float32(16)) / np.float32(np.log(128 / 16)) * np.float32(16)).astype(np.int32)
    large = np.minimum(large, 31)
    return np.where(d < 16, d, large)


def _slab_cols():
    q0, k0, v0, ga0, a0, b0, gc0 = 0, 1024, 1280, 1536, 2560, 3584, 4608
    cols = []
    for typ, idx in WSLABS:
        if typ == "G":
            c = list(range(a0 + idx * 128, a0 + (idx + 1) * 128)) + list(range(b0 + idx * 128, b0 + (idx + 1) * 128))
        elif typ == "Q":
            c = []
            for ci in range(2):
                ch = idx * 2 + ci
                m, i = ch // 4, ch % 4
                for h in (4 * (2 * m) + i, 4 * (2 * m + 1) + i):
                    c += list(range(q0 + h * 64, q0 + (h + 1) * 64))
        elif typ == "K":
            c = list(range(k0, k0 + 256))
        elif typ == "C":
            c = list(range(gc0 + idx * 256, gc0 + (idx + 1) * 256))
        elif typ == "V":
            c = list(range(v0, v0 + 256))
        else:
            c = list(range(ga0 + idx * 256, ga0 + (idx + 1) * 256))
        cols.append(c)
    return cols


def make_in_maps(x, c, w_ada, b_ada, w_in, rel_bias, sinks, conv_w, conv_b, conv_ln_g, conv_ln_b,
                 w_pw, b_pw, w_out, ln_g, ln_b):
    f = np.float32
    x2 = np.asarray(x, f).reshape(SEQ, D)
    wa0 = np.asarray(w_ada, f)[0]
    acols = []
    for j in range(16):
        acols.append(list(range(j * 128, (j + 1) * 128)) + list(range(2048 + j * 128, 2048 + (j + 1) * 128)))
    for j in range(8):
        acols.append(list(range(4096 + j * 256, 4096 + (j + 1) * 256)))
    wada = np.empty((24, 128, KC, 256), f)
    for j, cc in enumerate(acols):
        wada[j] = wa0[:, cc].reshape(KC, 128, 256).transpose(1, 0, 2)
    w_in0 = np.asarray(w_in, f)[0]
    wsl = np.empty((len(WSLABS), 128, KC, 256), f)
    for si, cols in enumerate(_slab_cols()):
        wsl[si] = w_in0[:, cols].reshape(KC, 128, 256).transpose(1, 0, 2)
    d = np.arange(128)
    onehot = (t5_bucket_np(d)[None, :] == np.arange(32)[:, None]).astype(f)
    wpw = np.ascontiguousarray(np.asarray(w_pw, f)[0].reshape(8, 128, 1024).transpose(1, 0, 2))
    wout = np.ascontiguousarray(np.asarray(w_out, f)[0].reshape(KC, 128, D).transpose(1, 0, 2))
    cp = np.zeros((128, 48 + 248 + 32 + 2), f)
    cp[:, 0:48] = np.asarray(b_ada, f)[0].reshape(48, 128).T
    cp[:, 48:296] = np.asarray(conv_w, f)[0].reshape(31, 8, 128).transpose(2, 1, 0).reshape(128, 248)
    cp[:, 296:304] = np.asarray(conv_b, f)[0].reshape(8, 128).T
    cp[:, 304:312] = np.asarray(conv_ln_g, f)[0].reshape(8, 128).T
    cp[:, 312:320] = np.asarray(conv_ln_b, f)[0].reshape(8, 128).T
    cp[:, 320:328] = np.asarray(b_pw, f)[0].reshape(8, 128).T
    ident = np.eye(128, dtype=f)
    shared = dict(cT=np.ascontiguousarray(np.asarray(c, f).reshape(KC, 128).T), wada=wada, wslabs=wsl, onehot=onehot,
                  rel_bias=np.ascontiguousarray(np.asarray(rel_bias, f)), sinks=np.asarray(sinks, f).reshape(16),
                  wpw=wpw, wout=wout, ln_g=np.asarray(ln_g, f).reshape(D), ln_b=np.asarray(ln_b, f).reshape(D),
                  ident=ident)
    maps = []
    for i in range(NCORES):
        s0 = i * T
        xh = np.zeros((TT, D), f)
        if i > 0:
            xh[:] = x2[s0 - HALO:s0 + T]
        else:
            xh[HALO:] = x2[0:T]
        xT = np.ascontiguousarray(xh.reshape(TT, KC, 128).transpose(2, 1, 0))
        cpi = cp.copy()
        cpi[:, 328] = 0.0 if i == 0 else 1.0
        cpi[:, 329] = NEG if i == 0 else 0.0
        m = dict(shared)
        m.update(xT=xT, xs=np.ascontiguousarray(x2[s0:s0 + T]), cpack=cpi)
        maps.append(m)
    return maps


_CACHE = {}


def kernel(x, c, w_ada, b_ada, w_in, rel_bias, sinks, conv_w, conv_b, conv_ln_g, conv_ln_b,
           w_pw, b_pw, w_out, ln_g, ln_b):
    debug = bool(os.environ.get("KDEBUG"))
    nc, _ = build(debug=debug)
    maps = make_in_maps(x, c, w_ada, b_ada, w_in, rel_bias, sinks, conv_w, conv_b, conv_ln_g, conv_ln_b,
                        w_pw, b_pw, w_out, ln_g, ln_b)
    sel = os.environ.get("KCORES")
    if sel:
        ids = [int(v) for v in sel.split(",")]
        res = run_bass_kernel_spmd(nc, [maps[i] for i in ids], core_ids=list(range(len(ids))))
        _CACHE["res"] = {i: r for i, r in zip(ids, res.results)}
        return None
    res = run_bass_kernel_spmd(nc, maps, core_ids=list(range(NCORES)))
    out = np.concatenate([r["out"] for r in res.results], axis=0).reshape(1, SEQ, D).astype(np.float32)
    if debug:
        _CACHE["res"] = res.results
    return out
```

```python
import os
import contextlib
import numpy as np
import concourse.bass as bass
import concourse.mybir as mybir
from concourse.bass_utils import run_bass_kernel_spmd

F32 = mybir.dt.float32
BF16 = mybir.dt.bfloat16
AF = mybir.ActivationFunctionType
ALU = mybir.AluOpType

D = 2048
SEQ = 8192
NCORES = 8
T = SEQ // NCORES
HALO = 128
TT = T + HALO
KC = D // 128
D_IN = 5632
NEG = -30000.0
ALPHA = 2.0 ** 0.25
EPS = 1e-5
NU = 4
UW = T + 32
NWB = 3
NSA = 24
NSA0 = 16
CONV_PER_UNIT = 3
CONV_DVE_SHARE = 0.85

SLABS = [("G", 0), ("Q", 0), ("M", 0), ("Q", 1), ("G", 1), ("M", 1), ("K", 0), ("V", 0),
         ("G", 2), ("A", 0), ("A", 1), ("M", 2), ("G", 3), ("Q", 2), ("Q", 3), ("M", 3),
         ("G", 4), ("G", 5), ("M", 4), ("A", 2), ("A", 3), ("M", 5), ("G", 6), ("M", 6),
         ("G", 7), ("M", 7), ("C", 0), ("C", 1), ("C", 2), ("C", 3)]
WSLABS = [s_ for s_ in SLABS if s_[0] != "M"]

_ESZ = {}


def esize(dt):
    if dt not in _ESZ:
        _ESZ[dt] = int(np.dtype(mybir.dt.np(dt)).itemsize)
    return _ESZ[dt]


class Prog:
    STREAMS = ("pe", "act", "dve", "pool", "sp")

    def __init__(self, nc, n_dma_sems=12):
        self.nc = nc
        self.ops = []
        self.base = {}
        self.n_dma_sems = n_dma_sems

    def sb(self, name, shape, dtype, offset):
        t = self.nc.alloc_sbuf_tensor_at(name, list(shape), dtype, offset=offset)
        self.base[t.name] = ("sb", offset)
        return t

    def psum(self, name, shape, dtype):
        t = self.nc.alloc_psum_tensor(name, list(shape), dtype)
        self.base[t.name] = ("ps", self.nc.lookup_mloc(t).addr)
        return t

    def dram(self, name, shape, dtype, kind="Internal"):
        t = self.nc.dram_tensor(name, list(shape), dtype, kind=kind)
        self.base[t.name] = ("dram:" + t.name, 0)
        return t

    def region(self, ap):
        space, base = self.base[ap.tensor.name]
        es = esize(ap.dtype)
        pairs = [tuple(p) for p in ap.ap]
        off = int(ap.offset)
        if space in ("sb", "ps"):
            pstride = int(np.prod(list(ap.tensor.shape)[1:]))
            foff = off % pstride
            ext = 1
            for st, cnt in pairs[1:]:
                ext += (cnt - 1) * abs(st)
            lo = base + foff * es
            return (space, lo, lo + ext * es)
        ext = 1
        for st, cnt in pairs:
            ext += (cnt - 1) * abs(st)
        return (space, off * es, (off + ext) * es)

    def op(self, stream, fn, reads=(), writes=(), dma=False, name=""):
        rr, ww = [], []
        for lst, is_w in ((reads, False), (writes, True)):
            for a in lst:
                sp, lo, hi = self.region(a)
                if sp == "ps":
                    assert lo // 2048 == (hi - 1) // 2048, ("psum access crosses a bank", lo, hi)
                    ww.append((sp, lo // 2048 * 2048, lo // 2048 * 2048 + 2048))
                elif is_w:
                    ww.append((sp, lo, hi))
                else:
                    rr.append((sp, lo, hi))
        self.ops.append(dict(stream=stream, fn=fn, reads=rr, writes=ww, dma=dma,
                             name=name, deps=set(), idx=len(self.ops)))
        return len(self.ops) - 1

    def _analyze(self):
        segs = {}

        def touch(space, lo, hi, opi, is_write, deps):
            lst = segs.setdefault(space, [])
            new, out = [], []
            cur = lo
            for s in lst:
                slo, shi, w, rd = s
                if shi <= lo or slo >= hi:
                    out.append(s)
                    continue
                if slo < lo:
                    out.append([slo, lo, w, list(rd)])
                a, b = max(slo, lo), min(shi, hi)
                if cur < a:
                    new.append([cur, a, None, []])
                new.append([a, b, w, list(rd)])
                cur = b
                if shi > hi:
                    out.append([hi, shi, w, list(rd)])
            if cur < hi:
                new.append([cur, hi, None, []])
            for s in new:
                if is_write:
                    if s[2] is not None:
                        deps.add((s[2], "waw"))
                    for r in s[3]:
                        deps.add((r, "war"))
                    s[2], s[3] = opi, []
                else:
                    if s[2] is not None:
                        deps.add((s[2], "raw"))
                    if opi not in s[3]:
                        s[3].append(opi)
            out.extend(new)
            out.sort(key=lambda x: x[0])
            merged = []
            for s in out:
                if merged and merged[-1][1] == s[0] and merged[-1][2] == s[2] and merged[-1][3] == s[3]:
                    merged[-1][1] = s[1]
                else:
                    merged.append(s)
            segs[space] = merged

        for o in self.ops:
            deps = set()
            for (sp, lo, hi) in o["reads"]:
                touch(sp, lo, hi, o["idx"], False, deps)
            for (sp, lo, hi) in o["writes"]:
                touch(sp, lo, hi, o["idx"], True, deps)
            final = set()
            for (d, kind) in deps:
                if d == o["idx"]:
                    continue
                do = self.ops[d]
                if (not do["dma"]) and (not o["dma"]) and do["stream"] == o["stream"]:
                    if o["stream"] == "pe":
                        continue
                final.add(d)
            o["deps"] = final

    def emit(self):
        nc = self.nc
        self._analyze()
        needed = set()
        for o in self.ops:
            needed |= o["deps"]
        stack = contextlib.ExitStack()
        sems = {s: stack.enter_context(nc.semaphore("sem_" + s)) for s in self.STREAMS}
        dma_sems = {}
        for s in self.STREAMS:
            if any(o["dma"] and o["stream"] == s for o in self.ops):
                dma_sems[s] = [stack.enter_context(nc.semaphore("dsem_%s_%d" % (s, i)))
                               for i in range(self.n_dma_sems)]
        cnt = {s: 0 for s in self.STREAMS}
        dcount = {s: [0] * self.n_dma_sems for s in dma_sems}
        dnext = {s: 0 for s in dma_sems}
        for o in self.ops:
            st = o["stream"]
            if o["dma"]:
                k = dnext[st] % self.n_dma_sems
                dnext[st] += 1
                o["prev_val"] = dcount[st][k]
                dcount[st][k] += 16
                o["sig"] = (("d", st, k), dcount[st][k])
            elif o["idx"] in needed:
                cnt[st] += 1
                o["sig"] = ((st,), cnt[st])
            else:
                o["sig"] = None

        def semof(key):
            return dma_sems[key[1]][key[2]] if key[0] == "d" else sems[key[0]]

        known = {s: {} for s in self.STREAMS}
        for o in self.ops:
            kn = known[o["stream"]]
            waits = []
            if o["dma"]:
                key, val = o["sig"]
                if o["prev_val"] > kn.get(key, 0):
                    waits.append((key, o["prev_val"]))
                    kn[key] = o["prev_val"]
            for d in sorted(o["deps"]):
                key, val = self.ops[d]["sig"]
                if kn.get(key, 0) >= val:
                    continue
                waits.append((key, val))
                kn[key] = val
                for k2, v2 in self.ops[d]["snap"].items():
                    if kn.get(k2, 0) < v2:
                        kn[k2] = v2
            best = {}
            for k, v in waits:
                best[k] = max(best.get(k, 0), v)
            o["waits"] = list(best.items())
            snap = dict(kn)
            if o["sig"] is not None and not o["dma"]:
                snap[o["sig"][0]] = max(snap.get(o["sig"][0], 0), o["sig"][1])
            o["snap"] = snap
        self.stats = {s: sum(1 for o in self.ops if o["stream"] == s) for s in self.STREAMS}
        self.nwaits = sum(len(o["waits"]) for o in self.ops)
        engmap = {"pe": "tensor", "act": "scalar", "dve": "vector", "pool": "gpsimd", "sp": "sync"}
        with stack:
            with nc.Block() as block:
                for s in self.STREAMS:
                    myops = [o for o in self.ops if o["stream"] == s]
                    if not myops:
                        continue

                    def body(eng, myops=myops):
                        for o in myops:
                            for (key, val) in o["waits"]:
                                eng.wait_ge(semof(key), val)
                            ins = o["fn"](eng)
                            if o["sig"] is not None:
                                ins.then_inc(semof(o["sig"][0]), 16 if o["dma"] else 1)
                    getattr(block, engmap[s])(body)


def fsz(t):
    return int(np.prod(list(t.shape)[1:]))


def A(t, foff, dims, p0=0, npart=128):
    F = fsz(t)
    return bass.AP(t, p0 * F + foff, [[F, npart]] + [list(d) for d in dims])


def build(debug=False):
    stop = int(os.environ.get('KSTOP', '9'))
    nc = bass.Bass("TRN2", target_bir_lowering=False)
    P = Prog(nc)
    xT_d = P.dram("xT", [128, KC, TT], F32, kind="ExternalInput")
    xs_d = P.dram("xs", [T, D], F32, kind="ExternalInput")
    cT_d = P.dram("cT", [128, KC], F32, kind="ExternalInput")
    wada_d = P.dram("wada", [NSA, 128, KC, 256], F32, kind="ExternalInput")
    NCP = 48 + 248 + 8 * 4 + 2
    cp_d = P.dram("cpack", [128, NCP], F32, kind="ExternalInput")
    wsl_d = P.dram("wslabs", [len(WSLABS), 128, KC, 256], F32, kind="ExternalInput")
    oh_d = P.dram("onehot", [32, 128], F32, kind="ExternalInput")
    rb_d = P.dram("rel_bias", [32, 16], F32, kind="ExternalInput")
    sk_d = P.dram("sinks", [16], F32, kind="ExternalInput")
    wpw_d = P.dram("wpw", [128, 8, 1024], F32, kind="ExternalInput")
    wout_d = P.dram("wout", [128, KC, D], F32, kind="ExternalInput")
    lng_d = P.dram("ln_g", [D], F32, kind="ExternalInput")
    lnb_d = P.dram("ln_b", [D], F32, kind="ExternalInput")
    id_d = P.dram("ident", [128, 128], F32, kind="ExternalInput")
    out_d = P.dram("out", [T, D], F32, kind="ExternalOutput")
    LB = 384
    bsc_d = P.dram("bias_scratch", [16 * 128 * LB + 512], F32)
    gsc_d = P.dram("gate_scratch", [D], F32)

    B0 = 16640
    LIMIT = 229376
    o_const = B0
    o_bias = o_const + 4096
    o_vT = o_bias + 16384
    o_sgc = o_vT + 32768
    o_hT = o_sgc + 16384
    o_wb = o_hT + 36864
    o_qT = o_wb + NWB * 8192
    o_kT = o_qT + 16384
    o_V1 = o_kT + 4608
    o_sga = o_V1 + 4768
    o_uT = o_sga + 16384
    o_sig = o_uT + NU * UW * 4
    o_att = o_sig + 4096
    o_end = o_att + 15360 + 512 + 2048
    assert o_end <= LIMIT, o_end

    _oc = [o_const]

    def cst(name, shape, dtype):
        nbytes = int(np.prod(shape[1:])) * esize(dtype)
        t_ = P.sb(name, shape, dtype, _oc[0])
        _oc[0] += (nbytes + 31) // 32 * 32
        return t_
    cpk = cst("cpk", [128, NCP], F32)
    modT = cst("modT", [128, 48], F32)
    s1T = cst("s1T", [128, 16], F32)
    esink = cst("esink", [128, 16], F32)
    identb = cst("identb", [128, 128], BF16)
    onesb = cst("onesb", [128, 128], BF16)
    den = cst("den", [128, 8], F32)
    rden = cst("rden", [128, 8], F32)
    bnst = cst("bnst", [128, 3, 24], F32)
    mv = cst("mv", [128, 3, 2], F32)
    sd = cst("sd", [128, 3, 2], F32)
    rbt = cst("rbt", [32, 16], F32)
    oht = cst("oht", [32, 128], F32)
    ct = cst("ct", [128, KC], F32)
    ctf = cst("ctf", [128, KC], F32)
    chl = cst("chl", [128, KC, 2], BF16)
    ones2 = cst("ones2", [2, 2], F32)
    assert _oc[0] <= o_const + 4096, _oc[0]
    C_BADA, C_CW, C_CB, C_LG, C_LB, C_BPW, C_FLAG = 0, 48, 296, 304, 312, 320, 328

    bhi = P.sb("bhi", [128, 2, 2048], BF16, o_bias)
    blo = P.sb("blo", [128, 2, 2048], BF16, o_bias + 8192)
    bias32 = P.sb("bias32", [128, 2, 2048], F32, o_sga)
    vT = P.sb("vT", [128, 8, T], F32, o_vT)
    yTa = P.sb("yTa", [128, 8, T], BF16, o_sgc)
    hT = P.sb("hT", [128, KC, TT], BF16, o_hT)
    yTc = P.sb("yTc", [128, 8, T], BF16, o_hT)
    wb = [P.sb("wb%d" % i, [128, KC, 256], BF16, o_wb + i * 8192) for i in range(NWB)]
    qT = P.sb("qT", [128, 8, T], BF16, o_qT)
    sgcT = P.sb("sgcT", [128, 8, T], BF16, o_qT)
    kT = P.sb("kT", [128, 2, TT], BF16, o_kT)
    V1 = P.sb("V1", [128, 9, 4, 66], BF16, o_V1)
    sga = P.sb("sga", [128, 8, 1024], BF16, o_sga)
    uT = [P.sb("uT%d" % i, [128, UW], F32, o_uT + i * UW * 4) for i in range(NU)]
    sig = [P.sb("sig%d" % i, [128, 512], F32, o_sig + i * 2048) for i in range(2)]
    NWA = 4
    wa = [P.sb("wa%d" % i, [128, KC, 256], BF16, o_qT + i * 8192) for i in range(NWA)]
    modrow = P.sb("modrow", [2, 4096], F32, o_qT + 32768)
    Eb = P.sb("Eb", [16, LB], F32, o_hT + 15 * TT * 2)
    xoff = [o_vT + i * 4608 for i in range(7)] + [o_sgc + i * 4608 for i in range(3)] + \
           [o_uT + 8544 + i * 4608 for i in range(6)]
    assert o_qT + 24576 + 24576 + LB * 4 <= o_uT + 8544 and o_uT + 8544 + 6 * 4608 <= o_end
    xst = [P.sb("xst%d" % i, [128, TT], F32, xoff[i]) for i in range(KC)]
    modrow_g = P.sb("modrow_g", [2, 2048], F32, o_att)
    PTt = [P.sb("PTt%d" % i, [128, 2, 512], BF16, o_att + 8192 + i * 2048) for i in range(2)] + \
          [P.sb("PTt2", [128, 2, 512], BF16, o_att + 15360 + 512)]
    otmp = [P.sb("otmp%d" % i, [128, 256], F32, o_att + 12288 + i * 1024) for i in range(2)]
    yatm = [P.sb("yatm%d" % i, [128, 256], BF16, o_att + 14336 + i * 512) for i in range(2)] + \
           [P.sb("yatm2", [128, 256], BF16, o_att + 15360)]
    zT = P.sb("zT", [128, 8, T], BF16, o_bias)
    wpwb = P.sb("wpwb", [128, 8, 1024], BF16, o_hT + 16384)
    assert 32768 <= 36864
    o_ln = o_att
    vbb = [P.sb("vbb%d" % i, [128, 512], BF16, o_ln + i * 1024) for i in range(2)]
    vsq = [P.sb("vsq%d" % i, [128, 512], BF16, o_ln + 2048 + i * 1024) for i in range(2)]
    mus = [P.sb("mu_sb%d" % i, [128, 512], F32, o_ln + 4096 + i * 2048) for i in range(2)]
    rss = [P.sb("rs_sb%d" % i, [128, 512], F32, o_sig + i * 2048) for i in range(2)]
    tnorm = [P.sb("tnorm%d" % i, [128, 512], F32, o_ln + 8192 + i * 2048) for i in range(3)]
    assert o_ln + 14336 <= o_att + 15360
    woutb_a = P.sb("woutb_a", [128, 6, D], BF16, o_wb)
    woutb_b = P.sb("woutb_b", [128, 10, D], BF16, o_kT)
    assert o_kT + 40960 <= o_sig

    def wout_kc(kc):
        return woutb_a[:, kc, :] if kc < 6 else woutb_b[:, kc - 6, :]

    def yT_kc(kc):
        return yTa[:, kc, :] if kc < 8 else yTc[:, kc - 8, :]
    gate_bc = P.sb("gate_bc", [128, D], F32, o_vT)
    lng_bc = P.sb("lng_bc", [128, D], F32, o_vT + 8192)
    lnb_bc = P.sb("lnb_bc", [128, D], F32, o_vT + 16384)
    xt4 = [P.sb("xt4_%d" % i, [128, D], F32, o_bias + i * 8192) for i in range(2)] + \
          [P.sb("xt4_2", [128, D], F32, o_vT + 24576)]
    r4 = [P.sb("r4_%d" % i, [128, D], F32, o_hT + 16384 + i * 8192) for i in range(2)] + \
         [P.sb("r4_2", [128, D], F32, o_uT + 15232)]
    o4 = [P.sb("o4_%d" % i, [128, D], F32, o_qT + i * 8192) for i in range(2)] + \
         [P.sb("o4_2", [128, D], F32, o_uT + 15232 + 8192)]
    assert o_kT + 40960 <= o_uT + 15232 and o_uT + 15232 + 16384 <= o_end

    psf = P.psum("psf", [128, 3584], F32)
    psb = P.psum("psb", [128, 1024], BF16)

    def dma(stream, out, in_, **kw):
        P.op(stream, lambda e: e.dma_start(out=out, in_=in_, **kw), reads=[in_], writes=[out], dma=True)

    def act(out, in_, func, bias=None, scale=None, accum_out=None):
        rd = [in_]
        kw = {}
        if accum_out is not None:
            kw["accum_out"] = accum_out
        if bias is not None:
            kw["bias"] = bias
            if not isinstance(bias, float):
                rd.append(bias)
        if scale is not None:
            kw["scale"] = scale
            if not isinstance(scale, float):
                rd.append(scale)
        P.op("act", lambda e: e.activation(out=out, in_=in_, func=func, **kw), reads=rd,
             writes=[out] + ([accum_out] if accum_out is not None else []))

    def tt(eng, out, a, b, op):
        P.op(eng, lambda e: e.tensor_tensor(out=out, in0=a, in1=b, op=op), reads=[a, b], writes=[out])

    def ts(eng, out, in0, s1, s2, op0, op1=None):
        rd = [in0] + [s for s in (s1, s2) if s is not None and not isinstance(s, float)]
        if op1 is None:
            P.op(eng, lambda e: e.tensor_scalar(out=out, in0=in0, scalar1=s1, scalar2=None, op0=op0),
                 reads=rd, writes=[out])
        else:
            P.op(eng, lambda e: e.tensor_scalar(out=out, in0=in0, scalar1=s1, scalar2=s2, op0=op0, op1=op1),
                 reads=rd, writes=[out])

    def stt(out, in0, scalar, in1, op0, op1):
        rd = [in0, in1] + ([] if isinstance(scalar, float) else [scalar])
        P.op("dve", lambda e: e.scalar_tensor_tensor(out=out, in0=in0, scalar=scalar, in1=in1,
                                                     op0=op0, op1=op1), reads=rd, writes=[out])

    def cpy(eng, out, in_):
        P.op(eng, lambda e: e.tensor_copy(out=out, in_=in_), reads=[in_], writes=[out])

    def mm_groups(groups, open_=True, close=True):
        rd, wr = [], []
        for out, pairs in groups:
            wr.append(out)
            for l, r in pairs:
                rd += [l, r]

        def fn(e):
            ins = None
            for out, pairs in groups:
                n = len(pairs)
                for i, (l, r) in enumerate(pairs):
                    ins = e.matmul(out, lhsT=l, rhs=r, start=(open_ and i == 0), stop=(close and i == n - 1),
                                   skip_group_check=not (open_ and close))
            return ins
        P.op("pe", fn, reads=rd, writes=wr)

    bankctr = [0]

    def next_bank():
        b = bankctr[0] % 4
        bankctr[0] += 1
        return b * 512

    dma("sp", cpk[:, :], cp_d.ap())
    dma("sp", ct[:, :], cT_d.ap())
    dma("sp", esink[:, :], bass.AP(sk_d, 0, [[0, 128], [1, 16]]))
    dma("sp", rbt[:, :], rb_d.ap())
    dma("sp", oht[:, :], oh_d.ap())
    dma("pool", identb[:, :], id_d.ap())
    assert SLABS[0] == ("G", 0)
    NPRE = NWA

    def xT_load(kc):
        dma("pool", xst[kc][:, :], bass.AP(xT_d, kc * TT, [[KC * TT, 128], [1, TT]]))
    for j in range(min(NPRE, NSA0)):
        dma("pool", wa[j % NWA][:, :, :], wada_d.ap()[j])
        xT_load(j)
    dma("pool", wb[0][:, :, :], wsl_d.ap()[0])
    act(ct[:, :], ct[:, :], AF.Silu)
    ts("dve", cpk[:, C_CW:C_CW + 248], cpk[:, C_CW:C_CW + 248], 0.5, None, ALU.mult)
    act(esink[:, :], esink[:, :], AF.Exp)
    P.op("dve", lambda e: e.memset(onesb[:, :], 1.0), writes=[onesb[:, :]])
    P.op("dve", lambda e: e.memset(ones2[:, :], 1.0), writes=[ones2[:, :]])
    P.op("dve", lambda e: e.memset(Eb[:, :], NEG), writes=[Eb[:, :]])
    cpy("dve", A(chl, 0, [[2, KC]]), ct[:, :])
    cpy("dve", ctf[:, :], A(chl, 0, [[2, KC]]))
    tt("dve", A(chl, 1, [[2, KC]]), ct[:, :], ctf[:, :], ALU.subtract)

    mm_groups([(A(psf, 4 * 512, [[1, 128]], 0, 16), [(rbt[:, :], oht[:, :])])])
    act(A(Eb, 128, [[1, 128]], 0, 16), A(psf, 4 * 512, [[1, 128]], 0, 16), AF.Identity)
    dma("sp", bass.AP(bsc_d, 0, [[128 * LB, 16], [LB, 128], [1, LB]]),
        bass.AP(Eb, 0, [[LB, 16], [0, 128], [1, LB]]))

    MOD_BANK = 6 * 512
    MT = MOD_BANK + 256
    EARLY = [(0, HALO, 512, 0), (1, HALO, 512, 512), (0, HALO + 512, 512, 1024), (1, HALO + 512, 512, 1536),
             (0, HALO - 32, 32, 2048), (1, HALO - 32, 32, 2560)]
    def emit_early(j):
        def early(e, j=j):
            ins = None
            for (ci, lo, n, col) in EARLY:
                ins = e.matmul(psf[:, col:col + n], lhsT=wb[0][:, j, ci * 128:(ci + 1) * 128], rhs=hT[:, j, lo:lo + n],
                               start=(j == 0), stop=(j == NSA0 - 1), skip_group_check=True)
            return ins
        P.op("pe", early, reads=[wb[0][:, j, :], hT[:, j, :]], writes=[psf[:, col:col + n] for (_, _, n, col) in EARLY])

    for j in range(NSA0):
        w_ = wa[j % NWA]
        bk = MOD_BANK
        mm_groups([(A(psf, bk, [[1, 256]], 0, 2), [(chl[:, kc, :], w_[:, kc, :]) for kc in range(KC)])])
        act(A(modrow, j * 256, [[1, 256]], 0, 2), A(psf, bk, [[1, 256]], 0, 2), AF.Identity)
        if j + NPRE < NSA0:
            dma("pool", wa[(j + NPRE) % NWA][:, :, :], wada_d.ap()[j + NPRE])
            xT_load(j + NPRE)
        def modtr(e, j=j):
            ins = None
            for h2 in range(2):
                ins = e.matmul(A(psf, MT + 2 * j + h2, [[1, 1]]),
                               lhsT=A(modrow, j * 256 + h2 * 128, [[1, 128]], 0, 2),
                               rhs=A(ones2, 0, [[1, 1]], 0, 2), start=True, stop=True)
            return ins
        P.op("pe", modtr, reads=[A(modrow, j * 256, [[1, 256]], 0, 2), ones2[:, :]],
             writes=[A(psf, MT + 2 * j, [[1, 2]])])
        if j >= 1:
            emit_early(j - 1)
        tt("dve", modT[:, j:j + 1], A(psf, MT + 2 * j, [[1, 1]]), cpk[:, C_BADA + j:C_BADA + j + 1], ALU.add)
        ts("dve", s1T[:, j:j + 1], A(psf, MT + 2 * j + 1, [[1, 1]]), cpk[:, C_BADA + 16 + j:C_BADA + 17 + j], 1.0,
           ALU.add, ALU.add)
        if j % 2 == 0:
            act(hT[:, j, :], xst[j][:, :], AF.Identity, bias=modT[:, j:j + 1], scale=s1T[:, j:j + 1])
        else:
            ts("dve", hT[:, j, :], xst[j][:, :], s1T[:, j:j + 1], modT[:, j:j + 1], ALU.mult, ALU.add)

    emit_early(NSA0 - 1)
    for pc in range(2):
        src = bass.AP(bsc_d, 256 - 128 * pc, [[LB - 1, 128], [128 * LB, 16], [1, 128]])
        dma("sp", A(bias32, pc * 2048, [[128, 16], [1, 128]]), src)
    P.op("dve", lambda e: e.memset(A(V1, 0, [[1, 9 * 4 * 66]]), 1.0), writes=[A(V1, 0, [[1, 9 * 4 * 66]])])

    conv_q = []

    def queue_conv(r, ub):
        acc = vT[:, r, :]
        act(acc, ub[:, 2:2 + T], AF.Identity, bias=cpk[:, C_CB + r:C_CB + r + 1],
            scale=cpk[:, C_CW + r * 31:C_CW + r * 31 + 1])
        for j in range(1, 31):
            conv_q.append(lambda j=j: stt(acc, ub[:, 2 + j:2 + j + T],
                                          cpk[:, C_CW + r * 31 + j:C_CW + r * 31 + j + 1], acc, ALU.mult, ALU.add))

    def drain_conv(n):
        for _ in range(min(n, len(conv_q))):
            conv_q.pop(0)()

    flag = cpk[:, C_FLAG:C_FLAG + 1]

    iters = [(g, b) for g in range(4) for b in range(8)]
    att_ready = set()
    S_BANK, O_BANK = 4 * 512, 6 * 512
    st_att = dict(s=0, pv=[], tr=[])

    def att_S(i):
        g, b = iters[i]
        m, s = g // 2, g % 2
        p0 = s * 64
        qap = A(qT, (4 * m) * T + b * 128, [[T, 4], [1, 128]], p0, 64)
        kprev = A(kT, m * TT + b * 128, [[1, 128]], p0, 64)
        kcur = A(kT, m * TT + (b + 1) * 128, [[1, 128]], p0, 64)
        o_p = A(psf, S_BANK, [[128, 4], [1, 128]])
        o_c = A(psf, S_BANK + 512, [[128, 4], [1, 128]])
        bp = lambda t_, pc: A(t_, pc * 2048 + g * 512, [[128, 4], [1, 128]])
        mm_groups([(o_p, [(identb[:, :], bp(bhi, 0)), (identb[:, :], bp(blo, 0))]),
                   (o_c, [(identb[:, :], bp(bhi, 1)), (identb[:, :], bp(blo, 1))])], close=False)
        mm_groups([(o_p, [(kprev, qap)]), (o_c, [(kcur, qap)])], open_=False)
        pt_ = PTt[i % 3]
        if b == 0:
            act(pt_[:, 0, :], psf[:, S_BANK:S_BANK + 512], AF.Exp, bias=cpk[:, C_FLAG + 1:C_FLAG + 2])
        else:
            act(pt_[:, 0, :], psf[:, S_BANK:S_BANK + 512], AF.Exp)
        act(pt_[:, 1, :], psf[:, S_BANK + 512:S_BANK + 1024], AF.Exp)

    def att_PV(i):
        g, b = iters[i]
        pt_ = PTt[i % 3]
        groups = []
        for j in range(4):
            groups.append((psf[:, O_BANK + j * 128:O_BANK + j * 128 + 65],
                           [(pt_[:, 0, j * 128:(j + 1) * 128], A(V1, b * 264 + g * 66, [[1, 65]])),
                            (pt_[:, 1, j * 128:(j + 1) * 128], A(V1, (b + 1) * 264 + g * 66, [[1, 65]]))]))
        mm_groups(groups)
        dn = A(den, (i % 2) * 4, [[1, 4]])
        rdn = A(rden, (i % 2) * 4, [[1, 4]])
        tt("dve", dn, A(psf, O_BANK + 64, [[128, 4]]), esink[:, 4 * g:4 * g + 4], ALU.add)
        P.op("dve", lambda e: e.reciprocal(out=rdn, in_=dn), reads=[dn], writes=[rdn])
        ot = otmp[i % 2]
        tt("dve", A(ot, 0, [[64, 4], [1, 64]]), A(psf, O_BANK, [[128, 4], [1, 64]]),
           A(rden, (i % 2) * 4, [[1, 4], [0, 64]]), ALU.mult)
        tt("dve", yatm[i % 3][:, :], ot[:, :], sga[:, b, g * 256:(g + 1) * 256], ALU.mult)

    def att_T(i):
        g, b = iters[i]
        ya = yatm[i % 3]
        off = (i % 2) * 256

        def tr(e):
            ins = None
            for c in range(2):
                ins = e.transpose(out=psb[:, off + c * 128:off + (c + 1) * 128], in_=ya[:, c * 128:(c + 1) * 128],
                                  identity=identb[:, :])
            return ins
        P.op("pe", tr, reads=[ya[:, :], identb[:, :]], writes=[psb[:, off:off + 256]])
        act(A(yTa, (2 * g) * T + b * 128, [[T, 2], [1, 128]]), A(psb, off, [[128, 2], [1, 128]]), AF.Identity)

    att_on = [True]

    def att_advance(allow_s=True):
        if stop < 2:
            return False
        did = False
        i_ = st_att["s"]
        can_s = allow_s and i_ < len(iters) and iters[i_][0] in att_ready
        if len(st_att["tr"]) > 1 or (st_att["tr"] and not st_att["pv"] and not can_s):
            att_T(st_att["tr"].pop(0)); did = True
        if len(st_att["pv"]) > 1 or (st_att["pv"] and not can_s):
            j = st_att["pv"].pop(0)
            att_PV(j); st_att["tr"].append(j); did = True
        if can_s:
            att_S(i_); st_att["pv"].append(i_); st_att["s"] += 1; did = True
        return did

    def unitT(w, ci, lo, n):
        bk = next_bank()
        out = psf[:, bk:bk + n]
        mm_groups([(out, [(w[:, kc, ci * 128:(ci + 1) * 128], hT[:, kc, lo:lo + n]) for kc in range(KC)])])
        return out

    credit = [0.0]

    conv_share = [CONV_DVE_SHARE]

    def after_unit(us=4.0):
        if us >= 2.0:
            att_advance(allow_s=att_on[0])
        credit[0] = min(credit[0] + us * conv_share[0], 12.0)
        while credit[0] >= 1.22 and conv_q:
            drain_conv(1)
            credit[0] -= 1.22

    widx = {}
    for s_ in SLABS:
        if s_[0] != "M":
            widx[s_] = len(widx)

    def slab_dma(si):
        if si < len(SLABS):
            typ_, idx_ = SLABS[si]
            src = wada_d.ap()[NSA0 + idx_] if typ_ == "M" else wsl_d.ap()[widx[(typ_, idx_)]]
            dma("pool", wb[si % NWB][:, :, :], src)

    def glu_evac(pa, pb, ub, n, c0):
        sg = sig[(bankctr[0] // 2) % 2]
        bankctr[0] += 2
        act(sg[:, 0:n], pb, AF.Tanh, scale=0.5)
        if n == 32:
            act(ub[:, 0:32], pa, AF.Identity, scale=flag)
        else:
            act(ub[:, c0:c0 + n], pa, AF.Identity)
        stt(ub[:, c0:c0 + n], sg[:, 0:n], 1.0, ub[:, c0:c0 + n], ALU.add, ALU.mult)

    for (lo, n, c0, col) in ((HALO, 512, 32, 0), (HALO + 512, 512, 32 + 512, 1024), (HALO - 32, 32, 0, 2048)):
        glu_evac(psf[:, col:col + n], psf[:, col + 512:col + 512 + n], uT[0], n, c0)
    queue_conv(0, uT[0])
    for si in range(1, min(NWB - 1, len(SLABS))):
        slab_dma(si)
    vslot = [0]
    for si, (typ, idx) in enumerate(SLABS):
        w = wb[si % NWB]
        slab_dma(si + NWB - 1)
        if si == 0:
            continue
        if si == 3:
            for pc in range(2):
                cpy("dve", bhi[:, pc, :], bias32[:, pc, :])
                tt("dve", blo[:, pc, :], bias32[:, pc, :], bhi[:, pc, :], ALU.subtract)
        att_on[0] = (typ != "A") and not (typ == "C" and idx >= 2)
        conv_share[0] = 1.05 if typ == "C" else 0.92
        if typ == "M":
            bk = next_bank()
            mm_groups([(A(psf, bk, [[1, 256]], 0, 2), [(chl[:, kc, :], w[:, kc, :]) for kc in range(KC)])])
            act(A(modrow_g, idx * 256, [[1, 256]], 0, 2), A(psf, bk, [[1, 256]], 0, 2), AF.Identity)
            after_unit(2.0)
            if idx == 7:
                bk = next_bank()

                def gtr(e, bk=bk):
                    ins = None
                    for jc in range(16):
                        ins = e.matmul(A(psf, bk + jc, [[1, 1]]), lhsT=A(modrow_g, jc * 128, [[1, 128]], 0, 2),
                                       rhs=A(ones2, 0, [[1, 1]], 0, 2), start=True, stop=True)
                    return ins
                P.op("pe", gtr, reads=[A(modrow_g, 0, [[1, 2048]], 0, 2), ones2[:, :]], writes=[A(psf, bk, [[1, 16]])])
                tt("dve", modT[:, 32:48], A(psf, bk, [[1, 16]]), cpk[:, C_BADA + 32:C_BADA + 48], ALU.add)
                dma("sp", bass.AP(gsc_d, 0, [[1, 128], [128, 16]]), modT[:, 32:48], allow_slow_non_contiguous=True)
        elif typ == "G":
            r = idx
            ub = uT[r % NU]
            for gi, (lo, n, c0) in enumerate(((HALO - 32, 32, 0), (HALO, 512, 32), (HALO + 512, 512, 32 + 512))):
                pa = unitT(w, 0, lo, n)
                if n == 32:
                    act(ub[:, 0:32], pa, AF.Identity, scale=flag)
                else:
                    act(ub[:, c0:c0 + n], pa, AF.Identity)
                after_unit(4.0 if n == 512 else 0.6)
                pb = unitT(w, 1, lo, n)
                sg = sig[(bankctr[0] // 2) % 2]
                act(sg[:, 0:n], pb, AF.Tanh, scale=0.5)
                stt(ub[:, c0:c0 + n], sg[:, 0:n], 1.0, ub[:, c0:c0 + n], ALU.add, ALU.mult)
                if gi == 2:
                    queue_conv(r, ub)
                after_unit(4.0 if n == 512 else 0.6)
        elif typ in ("Q", "K", "C"):
            for ci in range(2):
                ch = idx * 2 + ci
                for tg in range(2):
                    po = unitT(w, ci, HALO + tg * 512, 512)
                    cs = slice(tg * 512, (tg + 1) * 512)
                    if typ == "Q":
                        act(qT[:, ch, cs], po, AF.Identity, scale=0.125)
                    elif typ == "K":
                        act(kT[:, ci, HALO + tg * 512:HALO + (tg + 1) * 512], po, AF.Identity)
                    else:
                        act(sgcT[:, ch, cs], po, AF.Silu)
                    after_unit()
                if typ == "K":
                    po = unitT(w, ci, 0, HALO)
                    act(kT[:, ci, 0:HALO], po, AF.Identity)
                    after_unit(1.1)
        else:
            tiles = range(9) if typ == "V" else range(1, 9)
            for tt_i in tiles:
                bk = next_bank()
                pt = psf[:, bk:bk + 256]
                mm_groups([(pt, [(hT[:, kc, tt_i * 128:(tt_i + 1) * 128], w[:, kc, :]) for kc in range(KC)])])
                if typ == "V":
                    act(A(V1, tt_i * 264, [[66, 4], [1, 64]]), A(psf, bk, [[64, 4], [1, 64]]), AF.Identity)
                else:
                    act(sga[:, tt_i - 1, idx * 256:(idx + 1) * 256], pt, AF.Silu)
                after_unit(2.1)
            if typ == "A":
                att_ready.add(idx)
    att_on[0] = True
    while att_advance(True):
        drain_conv(CONV_PER_UNIT)
    drain_conv(10 ** 6)

    if stop >= 4:
        for kc in list(range(6, 12)):
            dma("pool", wout_kc(kc), wout_d.ap()[:, kc, :])
    if stop >= 3:
        for half in range(2):
            dma("pool", wpwb[:, half * 4:(half + 1) * 4, :], wpw_d.ap()[:, half * 4:(half + 1) * 4, :])
    if stop >= 4:
        for kc in list(range(12, 16)) + list(range(6)):
            dma("pool", wout_kc(kc), wout_d.ap()[:, kc, :])
    def t3_prep_stats_all():
        for c in range(8):
            for tg in range(2):
                tsl = slice(tg * 512, (tg + 1) * 512)
                pm = psf[:, tg * 1024:tg * 1024 + 512]
                pq = psf[:, tg * 1024 + 512:tg * 1024 + 1024]
                k_ = (c * 2 + tg) % 2
                act(vbb[k_][:, :], vT[:, c, tsl], AF.Identity)
                act(vsq[k_][:, :], vT[:, c, tsl], AF.Square)
                P.op("pe", lambda e, c=c, pm=pm, k_=k_: e.matmul(pm, lhsT=onesb[:, :], rhs=vbb[k_][:, :],
                                                                 start=(c == 0), stop=(c == 7), skip_group_check=True),
                     reads=[onesb[:, :], vbb[k_][:, :]], writes=[pm])
                P.op("pe", lambda e, c=c, pq=pq, k_=k_: e.matmul(pq, lhsT=onesb[:, :], rhs=vsq[k_][:, :],
                                                                 start=(c == 0), stop=(c == 7), skip_group_check=True),
                     reads=[onesb[:, :], vsq[k_][:, :]], writes=[pq])

    def t3_small(tg):
        pm = psf[:, tg * 1024:tg * 1024 + 512]
        pq = psf[:, tg * 1024 + 512:tg * 1024 + 1024]
        mu_, rs_ = mus[tg], rss[tg]
        act(mu_[:, :], pm, AF.Identity, scale=1.0 / 1024.0)
        stt(rs_[:, :], mu_[:, :], -1.0, mu_[:, :], ALU.mult, ALU.mult)
        stt(rs_[:, :], pq, 1.0 / 1024.0, rs_[:, :], ALU.mult, ALU.add)
        act(rs_[:, :], rs_[:, :], AF.Ln, bias=EPS)
        act(rs_[:, :], rs_[:, :], AF.Exp, scale=-0.5)

    def t3_norm(tg):
        tsl = slice(tg * 512, (tg + 1) * 512)
        mu_, rs_ = mus[tg], rss[tg]
        for c in range(8):
            eng = "pool" if c == 7 else "dve"
            tn = tnorm[c % 3]
            tt(eng, tn[:, :], vT[:, c, tsl], mu_[:, :], ALU.subtract)
            tt(eng, tn[:, :], tn[:, :], rs_[:, :], ALU.mult)
            act(zT[:, c, tsl], tn[:, :], AF.Silu, bias=cpk[:, C_LB + c:C_LB + c + 1], scale=cpk[:, C_LG + c:C_LG + c + 1])

    def t3_pw(tg):
        tsl = slice(tg * 512, (tg + 1) * 512)
        for f in range(8):
            k_ = (tg * 8 + f) % 2
            pp = psf[:, 2048 + k_ * 512:2048 + (k_ + 1) * 512]
            mm_groups([(pp, [(wpwb[:, kc, f * 128:(f + 1) * 128], zT[:, kc, tsl]) for kc in range(8)])])
            stt(yTc[:, f, tsl], pp, cpk[:, C_BPW + f:C_BPW + f + 1], sgcT[:, f, tsl], ALU.add, ALU.mult)

    if stop >= 3:
        t3_prep_stats_all()
        t3_small(0)
        t3_small(1)
        t3_norm(0)
        t3_norm(1)
        t3_pw(0)
        t3_pw(1)

    if stop >= 4:
        dma("sp", gate_bc[:, :], bass.AP(gsc_d, 0, [[0, 128], [1, D]]))
        dma("sp", lng_bc[:, :], bass.AP(lng_d, 0, [[0, 128], [1, D]]))
        dma("sp", lnb_bc[:, :], bass.AP(lnb_d, 0, [[0, 128], [1, D]]))
        for t8 in range(3):
            dma("sp", xt4[t8][:, :], xs_d.ap()[t8 * 128:(t8 + 1) * 128, :])
    hctr = [0]

    def t4_A(t8):
        xt_, r_ = xt4[t8 % 3], r4[t8 % 3]
        act(xt_[:, :], xt_[:, :], AF.Identity, scale=float(ALPHA))
        for hf in range(2):
            base = (hctr[0] % 3) * 1024
            hctr[0] += 1
            groups = []
            for n2 in range(2):
                ng = hf * 2 + n2
                groups.append((psf[:, base + n2 * 512:base + (n2 + 1) * 512],
                               [(yT_kc(kc)[:, t8 * 128:(t8 + 1) * 128], wout_kc(kc)[:, ng * 512:(ng + 1) * 512])
                                for kc in range(KC)]))
            mm_groups(groups)
            hs = slice(hf * 1024, (hf + 1) * 1024)
            for n2 in range(2):
                cs = slice(hf * 1024 + n2 * 512, hf * 1024 + (n2 + 1) * 512)
                tt("dve", r_[:, cs], psf[:, base + n2 * 512:base + (n2 + 1) * 512], gate_bc[:, cs], ALU.mult)
            tt("dve", r_[:, hs], r_[:, hs], xt_[:, hs], ALU.add)
            for q4 in (2 * hf, 2 * hf + 1):
                P.op("dve", lambda e, q4=q4, r_=r_, i2=t8 % 3: e.bn_stats(out=bnst[:, i2, q4 * 6:(q4 + 1) * 6],
                                                                         in_=r_[:, q4 * 512:(q4 + 1) * 512]),
                     reads=[r_[:, q4 * 512:(q4 + 1) * 512]], writes=[bnst[:, t8 % 3, q4 * 6:(q4 + 1) * 6]])
        if t8 + 3 < 8:
            dma("sp", xt4[t8 % 3][:, :], xs_d.ap()[(t8 + 3) * 128:(t8 + 4) * 128, :])
        i2 = t8 % 3
        P.op("dve", lambda e, i2=i2: e.bn_aggr(out=mv[:, i2, :], in_=bnst[:, i2, :]),
             reads=[bnst[:, i2, :]], writes=[mv[:, i2, :]])
        act(sd[:, i2, 0:1], mv[:, i2, 1:2], AF.Sqrt, bias=EPS)
        P.op("dve", lambda e, i2=i2: e.reciprocal(out=sd[:, i2, 0:1], in_=sd[:, i2, 0:1]),
             reads=[sd[:, i2, 0:1]], writes=[sd[:, i2, 0:1]])
        stt(sd[:, i2, 1:2], mv[:, i2, 0:1], -1.0, sd[:, i2, 0:1], ALU.mult, ALU.mult)

    def t4_B(t8):
        r_, o_ = r4[t8 % 3], o4[t8 % 3]
        i2 = t8 % 3
        act(o_[:, :], r_[:, :], AF.Identity, bias=sd[:, i2, 1:2], scale=sd[:, i2, 0:1])
        PS = 384 if t8 < 7 else 256
        tt("pool", o_[:, 0:PS], o_[:, 0:PS], lng_bc[:, 0:PS], ALU.mult)
        tt("dve", o_[:, PS:D], o_[:, PS:D], lng_bc[:, PS:D], ALU.mult)
        tt("pool", o_[:, 0:PS], o_[:, 0:PS], lnb_bc[:, 0:PS], ALU.add)
        tt("dve", o_[:, PS:D], o_[:, PS:D], lnb_bc[:, PS:D], ALU.add)
        if t8 < 7:
            dma("pool", out_d.ap()[t8 * 128:(t8 + 1) * 128, :], o_[:, :])
        else:
            dma("sp", out_d.ap()[t8 * 128:(t8 + 1) * 128, PS:D], o_[:, PS:D])
            dma("pool", out_d.ap()[t8 * 128:(t8 + 1) * 128, 0:PS], o_[:, 0:PS])

    if stop >= 4:
        t4_A(0)
        for t8 in range(1, 8):
            t4_A(t8)
            t4_B(t8 - 1)
        t4_B(7)

    dbg = []
    if debug:
        def dump(name, t):
            shp = list(t.shape)
            d_ = P.dram("dbg_" + name, shp, t.dtype, kind="ExternalOutput")
            src = A(t, 0, [[1, fsz(t)]], 0, shp[0])
            dst = bass.AP(d_, 0, [[fsz(t), shp[0]], [1, fsz(t)]])
            dma("sp", dst, src)
            dbg.append(d_)
        for nm, t in debug_targets(locals()):
            dump(nm, t)
    P.op("sp", lambda e: e.nop(), reads=[out_d.ap()] + [d_.ap() for d_ in dbg])
    P.emit()
    return nc, P


def debug_targets(loc):
    names = os.environ.get("KDEBUG", "").split(",")
    return [(n, loc[n]) for n in names if n and n in loc]


def t5_bucket_np(d):
    d = np.asarray(d)
    dd = np.maximum(d, 1).astype(np.float32)
    large = 16 + (np.log(dd / np.float32(16)) / np.float32(np.log(128 / 16)) * np.float32(16)).astype(np.int32)
    large = np.minimum(large, 31)
    return np.where(d < 16, d, large)


def _slab_cols():
    q0, k0, v0, ga0, a0, b0, gc0 = 0, 1024, 1280, 1536, 2560, 3584, 4608
    cols = []
    for typ, idx in WSLABS:
        if typ == "G":
            c = list(range(a0 + idx * 128, a0 + (idx + 1) * 128)) + list(range(b0 + idx * 128, b0 + (idx + 1) * 128))
        elif typ == "Q":
            c = []
            for ci in range(2):
                ch = idx * 2 + ci
                m, i = ch // 4, ch % 4
                for h in (4 * (2 * m) + i, 4 * (2 * m + 1) + i):
                    c += list(range(q0 + h * 64, q0 + (h + 1) * 64))
        elif typ == "K":
            c = list(range(k0, k0 + 256))
        elif typ == "C":
            c = list(range(gc0 + idx * 256, gc0 + (idx + 1) * 256))
        elif typ == "V":
            c = list(range(v0, v0 + 256))
        else:
            c = list(range(ga0 + idx * 256, ga0 + (idx + 1) * 256))
        cols.append(c)
    return cols


def make_in_maps(x, c, w_ada, b_ada, w_in, rel_bias, sinks, conv_w, conv_b, conv_ln_g, conv_ln_b,
                 w_pw, b_pw, w_out, ln_g, ln_b):
    f = np.float32
    x2 = np.asarray(x, f).reshape(SEQ, D)
    wa0 = np.asarray(w_ada, f)[0]
    acols = []
    for j in range(16):
        acols.append(list(range(j * 128, (j + 1) * 128)) + list(range(2048 + j * 128, 2048 + (j + 1) * 128)))
    for j in range(8):
        acols.append(list(range(4096 + j * 256, 4096 + (j + 1) * 256)))
    wada = np.empty((24, 128, KC, 256), f)
    for j, cc in enumerate(acols):
        wada[j] = wa0[:, cc].reshape(KC, 128, 256).transpose(1, 0, 2)
    w_in0 = np.asarray(w_in, f)[0]
    wsl = np.empty((len(WSLABS), 128, KC, 256), f)
    for si, cols in enumerate(_slab_cols()):
        wsl[si] = w_in0[:, cols].reshape(KC, 128, 256).transpose(1, 0, 2)
    d = np.arange(128)
    onehot = (t5_bucket_np(d)[None, :] == np.arange(32)[:, None]).astype(f)
    wpw = np.ascontiguousarray(np.asarray(w_pw, f)[0].reshape(8, 128, 1024).transpose(1, 0, 2))
    wout = np.ascontiguousarray(np.asarray(w_out, f)[0].reshape(KC, 128, D).transpose(1, 0, 2))
    cp = np.zeros((128, 48 + 248 + 32 + 2), f)
    cp[:, 0:48] = np.asarray(b_ada, f)[0].reshape(48, 128).T
    cp[:, 48:296] = np.asarray(conv_w, f)[0].reshape(31, 8, 128).transpose(2, 1, 0).reshape(128, 248)
    cp[:, 296:304] = np.asarray(conv_b, f)[0].reshape(8, 128).T
    cp[:, 304:312] = np.asarray(conv_ln_g, f)[0].reshape(8, 128).T
    cp[:, 312:320] = np.asarray(conv_ln_b, f)[0].reshape(8, 128).T
    cp[:, 320:328] = np.asarray(b_pw, f)[0].reshape(8, 128).T
    ident = np.eye(128, dtype=f)
    shared = dict(cT=np.ascontiguousarray(np.asarray(c, f).reshape(KC, 128).T), wada=wada, wslabs=wsl, onehot=onehot,
                  rel_bias=np.ascontiguousarray(np.asarray(rel_bias, f)), sinks=np.asarray(sinks, f).reshape(16),
                  wpw=wpw, wout=wout, ln_g=np.asarray(ln_g, f).reshape(D), ln_b=np.asarray(ln_b, f).reshape(D),
                  ident=ident)
    maps = []
    for i in range(NCORES):
        s0 = i * T
        xh = np.zeros((TT, D), f)
        if i > 0:
            xh[:] = x2[s0 - HALO:s0 + T]
        else:
            xh[HALO:] = x2[0:T]
        xT = np.ascontiguousarray(xh.reshape(TT, KC, 128).transpose(2, 1, 0))
        cpi = cp.copy()
        cpi[:, 328] = 0.0 if i == 0 else 1.0
        cpi[:, 329] = NEG if i == 0 else 0.0
        m = dict(shared)
        m.update(xT=xT, xs=np.ascontiguousarray(x2[s0:s0 + T]), cpack=cpi)
        maps.append(m)
    return maps


_CACHE = {}


def kernel(x, c, w_ada, b_ada, w_in, rel_bias, sinks, conv_w, conv_b, conv_ln_g, conv_ln_b,
           w_pw, b_pw, w_out, ln_g, ln_b):
    debug = bool(os.environ.get("KDEBUG"))
    nc, _ = build(debug=debug)
    maps = make_in_maps(x, c, w_ada, b_ada, w_in, rel_bias, sinks, conv_w, conv_b, conv_ln_g, conv_ln_b,
                        w_pw, b_pw, w_out, ln_g, ln_b)
    sel = os.environ.get("KCORES")
    if sel:
        ids = [int(v) for v in sel.split(",")]
        res = run_bass_kernel_spmd(nc, [maps[i] for i in ids], core_ids=list(range(len(ids))))
        _CACHE["res"] = {i: r for i, r in zip(ids, res.results)}
        return None
    res = run_bass_kernel_spmd(nc, maps, core_ids=list(range(NCORES)))
    out = np.concatenate([r["out"] for r in res.results], axis=0).reshape(1, SEQ, D).astype(np.float32)
    if debug:
        _CACHE["res"] = res.results
    return out
```
